# Optimizing a Trainium2 kernel written in Bass

```python
import jax, jax.numpy as jnp
from jax import lax
import numpy as np

D_MODEL = 4096
BATCH = 1
SEQ = 8192
DEPTH = 1

CHUNK = 64
Q_BLOCK = 128

D_MIX = D_MODEL
GLA_WIDTH = D_MIX // 2
GLA_HEADS = 16
GLA_DV = GLA_WIDTH // GLA_HEADS
GLA_DK = GLA_DV // 2
GLA_GATE_RANK = 16
GLA_GATE_TAU = 16.0

MLA_WIDTH = D_MIX - GLA_WIDTH
MLA_HEADS = 16
MLA_DV = MLA_WIDTH // MLA_HEADS
MLA_DN = 128
MLA_DR = 64
MLA_Q_RANK = 1536
MLA_KV_RANK = 512
ROPE_THETA = 10000.0

IN_SPLITS = (
    GLA_HEADS * GLA_DK,
    GLA_HEADS * GLA_DK,
    GLA_WIDTH,
    GLA_GATE_RANK,
    GLA_WIDTH,
    MLA_Q_RANK,
    MLA_KV_RANK,
    MLA_DR,
    MLA_WIDTH,
)
IN_WIDTH = sum(IN_SPLITS)

EPS = 1e-6

kernel_name = "hybrid_gla_mla_adaln_sandwich"


def rms_norm(t, g):
    tf = t.astype(jnp.float32)
    y = tf * lax.rsqrt(jnp.mean(tf * tf, axis=-1, keepdims=True) + EPS)
    return (y * g.astype(jnp.float32)).astype(t.dtype)


def apply_rope(t, cos, sin):
    tf = t.astype(jnp.float32)
    t1, t2 = jnp.split(tf, 2, axis=-1)
    return jnp.concatenate([t1 * cos - t2 * sin, t2 * cos + t1 * sin], axis=-1).astype(t.dtype)


def gla_chunk_causal(q, k, v, log_a):
    B, S, H, DK = q.shape
    DV = v.shape[-1]
    N = S // CHUNK
    f32 = jnp.float32
    qc = q.reshape(B, N, CHUNK, H, DK).astype(f32)
    kc = k.reshape(B, N, CHUNK, H, DK).astype(f32)
    vc = v.reshape(B, N, CHUNK, H, DV).astype(f32)
    la = log_a.reshape(B, N, CHUNK, H, DK).astype(f32)
    lcum = jnp.cumsum(la, axis=2)
    ltot = lcum[:, :, -1]
    k_dec = kc * jnp.exp(ltot[:, :, None] - lcum)
    q_dec = qc * jnp.exp(ltot)[:, :, None]
    scores = jnp.einsum('bnchd,bnshd->bnhcs', qc, k_dec)
    o_intra = jnp.einsum('bnhcs,bnshv->bnchv', scores, vc)
    kv = jnp.einsum('bnshd,bnshv->bnhdv', k_dec, vc)

    def step(state, inp):
        decay, kv_n = inp
        return decay[..., None] * state + kv_n, state

    s0 = jnp.zeros((B, H, DK, DV), f32)
    _, s_before = lax.scan(step, s0, (jnp.exp(ltot).swapaxes(0, 1), kv.swapaxes(0, 1)))
    s_before = s_before.swapaxes(0, 1)
    o_inter = jnp.einsum('bnchd,bnhdv->bnchv', q_dec, s_before)
    return (o_intra + o_inter).reshape(B, S, H, DV).astype(v.dtype)


def mla_chunk_causal(q_nope, q_rope, k_nope, k_rope, v):
    B, S, H, DN = q_nope.shape
    DR = q_rope.shape[-1]
    nb = S // Q_BLOCK
    scale = (DN + DR) ** -0.5
    key_chunk = jnp.arange(S) // CHUNK

    def block(args):
        qn, qr, i = args
        s = (jnp.einsum('bqhd,bkhd->bhqk', qn, k_nope)
             + jnp.einsum('bqhr,bkr->bhqk', qr, k_rope)).astype(jnp.float32) * scale
        q_chunk = (i * Q_BLOCK + jnp.arange(Q_BLOCK)) // CHUNK
        mask = key_chunk[None, :] <= q_chunk[:, None]
        s = jnp.where(mask[None, None], s, -jnp.inf)
        p = jax.nn.softmax(s, axis=-1).astype(v.dtype)
        return jnp.einsum('bhqk,bkhv->bqhv', p, v)

    qn_b = q_nope.reshape(B, nb, Q_BLOCK, H, DN).swapaxes(0, 1)
    qr_b = q_rope.reshape(B, nb, Q_BLOCK, H, DR).swapaxes(0, 1)
    out = lax.map(block, (qn_b, qr_b, jnp.arange(nb)))
    return out.swapaxes(0, 1).reshape(B, S, H, v.shape[-1])


def setup_inputs(seed: int = 0) -> dict:
    key = jax.random.key(seed)
    ks = jax.random.split(key, 20)
    f32 = jnp.float32
    nrm = lambda k, shape, s: jax.random.normal(k, shape, f32) * s
    x = nrm(ks[0], (BATCH, SEQ, D_MODEL), 1.0)
    c = nrm(ks[1], (BATCH, D_MODEL), 1.0)
    offset = jax.random.randint(ks[2], (BATCH, 1), 0, 4096, dtype=jnp.int32)
    positions = (offset + jnp.arange(SEQ, dtype=jnp.int32)[None, :]).astype(jnp.int32)
    w_ada = nrm(ks[3], (D_MODEL, 3 * D_MODEL), 0.5 * D_MODEL ** -0.5)
    b_ada = nrm(ks[4], (3 * D_MODEL,), 0.01)
    g_pre = 1.0 + nrm(ks[5], (D_MODEL,), 0.02)
    g_post = 1.0 + nrm(ks[6], (D_MODEL,), 0.02)
    w_in = nrm(ks[7], (D_MODEL, IN_WIDTH), D_MODEL ** -0.5)
    w_alpha_up = nrm(ks[8], (GLA_GATE_RANK, GLA_HEADS * GLA_DK), GLA_GATE_RANK ** -0.5)
    b_alpha = nrm(ks[9], (GLA_HEADS * GLA_DK,), 0.1)
    g_gla_out = 1.0 + nrm(ks[10], (GLA_DV,), 0.02)
    g_q_norm = 1.0 + nrm(ks[11], (MLA_Q_RANK,), 0.02)
    w_uq = nrm(ks[12], (MLA_Q_RANK, MLA_HEADS * (MLA_DN + MLA_DR)), MLA_Q_RANK ** -0.5)
    g_kv_norm = 1.0 + nrm(ks[13], (MLA_KV_RANK,), 0.02)
    w_ukv = nrm(ks[14], (MLA_KV_RANK, MLA_HEADS * (MLA_DN + MLA_DV)), MLA_KV_RANK ** -0.5)
    w_out = nrm(ks[15], (D_MIX, D_MODEL), D_MIX ** -0.5)
    return {"x": x, "c": c, "positions": positions, "w_ada": w_ada, "b_ada": b_ada,
            "g_pre": g_pre, "g_post": g_post, "w_in": w_in, "w_alpha_up": w_alpha_up,
            "b_alpha": b_alpha, "g_gla_out": g_gla_out, "g_q_norm": g_q_norm, "w_uq": w_uq,
            "g_kv_norm": g_kv_norm, "w_ukv": w_ukv, "w_out": w_out}


def reference(x, c, positions, w_ada, b_ada, g_pre, g_post, w_in, w_alpha_up, b_alpha,
              g_gla_out, g_q_norm, w_uq, g_kv_norm, w_ukv, w_out):
    B, S, D = x.shape
    half = MLA_DR // 2
    inv_freq = ROPE_THETA ** (-jnp.arange(half, dtype=jnp.float32) / half)
    ang = positions.astype(jnp.float32)[..., None] * inv_freq
    cos, sin = jnp.cos(ang), jnp.sin(ang)
    offsets = [int(o) for o in np.cumsum(IN_SPLITS)[:-1]]

    for _ in range(DEPTH):
        mod = jax.nn.silu(c) @ w_ada + b_ada
        shift, scale, gate = jnp.split(mod, 3, axis=-1)
        h = rms_norm(x, g_pre) * (1.0 + scale[:, None]) + shift[:, None]

        proj = h @ w_in
        (g_q, g_k, g_v, g_alr, g_gate,
         m_cq, m_ckv, m_kr, m_gate) = jnp.split(proj, offsets, axis=-1)

        q_a = g_q.reshape(B, S, GLA_HEADS, GLA_DK) * (GLA_DK ** -0.5)
        k_a = g_k.reshape(B, S, GLA_HEADS, GLA_DK)
        v_a = g_v.reshape(B, S, GLA_HEADS, GLA_DV)
        log_a = jax.nn.log_sigmoid((g_alr @ w_alpha_up + b_alpha).astype(jnp.float32)) / GLA_GATE_TAU
        log_a = log_a.reshape(B, S, GLA_HEADS, GLA_DK)
        o_a = gla_chunk_causal(q_a, k_a, v_a, log_a)
        o_a = rms_norm(o_a, g_gla_out).reshape(B, S, GLA_WIDTH) * jax.nn.silu(g_gate)

        q_b = (rms_norm(m_cq, g_q_norm) @ w_uq).reshape(B, S, MLA_HEADS, MLA_DN + MLA_DR)
        q_nope, q_rope = q_b[..., :MLA_DN], q_b[..., MLA_DN:]
        q_rope = apply_rope(q_rope, cos[:, :, None, :], sin[:, :, None, :])
        kv_b = (rms_norm(m_ckv, g_kv_norm) @ w_ukv).reshape(B, S, MLA_HEADS, MLA_DN + MLA_DV)
        k_nope, v_b = kv_b[..., :MLA_DN], kv_b[..., MLA_DN:]
        k_rope = apply_rope(m_kr, cos, sin)
        o_b = mla_chunk_causal(q_nope, q_rope, k_nope, k_rope, v_b)
        o_b = o_b.reshape(B, S, MLA_WIDTH) * jax.nn.silu(m_gate)

        mix = jnp.concatenate([o_a, o_b], axis=-1) @ w_out
        x = x + gate[:, None] * rms_norm(mix, g_post)
    return x
```

```python
import numpy as np
from contextlib import ExitStack
import concourse.bass as bass
import concourse.mybir as mybir
from concourse.bass_utils import run_bass_kernel_spmd

F32 = mybir.dt.float32
BF16 = mybir.dt.bfloat16
I32 = mybir.dt.int32
AF = mybir.ActivationFunctionType
ALU = mybir.AluOpType

D = 4096
KC = 32
BLK = 1024
EPS = 1e-6
NH = 16
TWO_PI = float(2 * np.pi)
C1 = 6.28125
C2 = float(2 * np.pi - 6.28125)
NEG = -30000.0


class SemGroup:
    __slots__ = ("name", "sem", "cnt")

    def __init__(self, name):
        self.name = name
        self.sem = None
        self.cnt = 0


class Buf:
    __slots__ = ("name", "writer", "readers", "grp")

    def __init__(self, name, grp=None):
        self.name = name
        self.writer = None
        self.readers = []
        self.grp = grp if grp is not None else SemGroup(name)


class Sched:
    def __init__(self, nc, stack):
        self.nc = nc
        self.stack = stack
        self.engs = {}
        for name, h in (("pe", nc.tensor), ("act", nc.scalar), ("dve", nc.vector),
                        ("pool", nc.gpsimd), ("sp", nc.sync)):
            sem = stack.enter_context(nc.semaphore("prog_" + name))
            self.engs[name] = dict(h=h, sem=sem, cnt=0, waited={})
        self.nsem = 0
        self.dmabufs = []

    def _wait(self, e, tok):
        sem, val, grp = tok
        if grp is not None:
            val = grp.cnt
        E = self.engs[e]
        key = id(sem)
        if E["waited"].get(key, 0) >= val:
            return
        E["h"].wait_ge(sem, val)
        E["waited"][key] = val

    def _deps(self, e, reads, writes):
        for b in reads:
            if b.writer is not None:
                self._wait(e, b.writer)
        for b in writes:
            if b.writer is not None:
                self._wait(e, b.writer)
            for t in b.readers:
                self._wait(e, t)

    def _commit(self, tok, reads, writes):
        for b in reads:
            b.readers.append(tok)
            if len(b.readers) > 12:
                best = {}
                for s, v, g in b.readers:
                    k = id(s)
                    if k not in best or best[k][1] < v:
                        best[k] = (s, v, g)
                b.readers = list(best.values())
        for b in writes:
            b.writer = tok
            b.readers = []

    def op(self, e, fn, reads=(), writes=()):
        self._deps(e, reads, writes)
        E = self.engs[e]
        ins = fn(E["h"])
        E["cnt"] += 1
        ins.then_inc(E["sem"], 1)
        tok = (E["sem"], E["cnt"], None)
        self._commit(tok, reads, writes)
        return tok

    def dma(self, q, fn, reads, writes):
        self._deps(q, reads, writes)
        own = writes[0]
        g = own.grp
        if own not in self.dmabufs:
            self.dmabufs.append(own)
        if g.sem is None:
            g.sem = self.stack.enter_context(self.nc.semaphore("d%d" % self.nsem))
            self.nsem += 1
        inss = fn(self.engs[q]["h"])
        if not isinstance(inss, (list, tuple)):
            inss = [inss]
        for ins in inss:
            ins.then_inc(g.sem, 16)
            g.cnt += 16
        tok = (g.sem, g.cnt, g)
        self._commit(tok, reads, writes)
        return tok

    def barrier(self):
        toks = [(E["sem"], E["cnt"], None) for E in self.engs.values() if E["cnt"] > 0]
        grps = {}
        for b in self.dmabufs:
            if b.grp.sem is not None:
                grps[id(b.grp)] = b.grp
        for e in self.engs:
            for t in toks:
                self._wait(e, t)
            for g in grps.values():
                self._wait(e, (g.sem, g.cnt, g))

    def wait_all(self, e, bufs):
        for b in bufs:
            if b.writer is not None:
                self._wait(e, b.writer)


def build(nblk, dev=False, stop=99):
    L = nblk * BLK
    NT = L // 128
    NS = L // 512
    NCH = L // 64
    OWN0 = L - BLK
    nc = bass.Bass("TRN2", target_bir_lowering=False)

    def din(name, shape, dt=F32):
        return nc.dram_tensor(name, list(shape), dt, kind="ExternalInput").ap()

    def dscr(name, shape, dt=BF16):
        return nc.dram_tensor(name, list(shape), dt,
                              kind=("ExternalOutput" if dev else "Internal")).ap()

    xl = din("xl", [L, D])
    pos64 = din("pos64", [64, L], I32)
    valid_d = din("valid", [128, NT])
    maskb_d = din("maskb", [128, NT])
    cvec = din("cvec", [128, KC])
    bada = din("bada", [1, 3 * D])
    gpre = din("gpre", [1, D])
    gpost = din("gpost", [1, D])
    wada = din("wada", [48, 128, KC, 256])
    wkfm = din("wkfm", [5, 128, KC, 128])
    walr_d = din("walr", [128, KC, 16])
    wktm = din("wktm", [12, 128, KC, 256])
    wqfm = din("wqfm", [52, 128, KC, 128])
    wuq = din("wuq", [NH, 128, 12, 256])
    wukv = din("wukv", [NH, 128, 4, 256])
    wout = din("wout", [8, 128, KC, 512])
    wup_d = din("wup", [17, 1024])
    gq_d = din("gq", [128, 12])
    gkv_d = din("gkv", [128, 4])
    ggla_d = din("ggla", [128, 1])
    invf_d = din("invf", [64, 1])
    sgn_d = din("sgn", [64, 1])
    ident_d = din("ident", [128, 128])
    tri_d = din("tri", [128, 128])
    cind_d = din("cind", [128, 2])
    y = nc.dram_tensor("y", [BLK, D], F32, kind="ExternalOutput").ap()

    s_wkfm = nc.dram_tensor("s_wkfm", [5, 128, KC, 128], BF16).ap()
    s_wktm = nc.dram_tensor("s_wktm", [12, 128, KC, 256], BF16).ap()
    s_wqfm = nc.dram_tensor("s_wqfm", [52, 128, KC, 128], BF16).ap()
    s_wuq = nc.dram_tensor("s_wuq", [NH, 128, 12, 256], BF16).ap()
    s_wukv = nc.dram_tensor("s_wukv", [NH, 128, 4, 256], BF16).ap()
    s_wout = nc.dram_tensor("s_wout", [8, 128, KC, 512], BF16).ap()
    s_lat = dscr("s_lat", [5, 128, L])
    s_Ssn = dscr("s_Ssn", [16, 128, 8 * 128])
    s_cqn = dscr("s_cqn", [12, 128, BLK])
    s_mg = dscr("s_mg", [16, 128, BLK])
    s_oT = dscr("s_oT", [KC, 128, BLK])
    s_gg = nc.dram_tensor("s_gg", [1, D], F32).ap()

    with ExitStack() as st:
        S = Sched(nc, st)

        def sbuf(stack, name, shape, dt):
            return stack.enter_context(nc.sbuf_tensor("t_" + name, list(shape), dt))

        def cast2d(ap, n):
            names = " ".join("a%d" % i for i in range(n))
            flat = ap.rearrange("%s -> (%s)" % (names, names))
            return flat.rearrange("(r c) -> r c", c=2048)

        P = {}
        rot = {"i": 0}

        def psum_alloc(stack, n32, n16, nrot):
            P["pb"] = [stack.enter_context(nc.psum_tensor("pb%d_%d" % (rot["i"], i), [128, 512], F32)) for i in range(n32)]
            P["Bpb"] = [Buf("pb%d" % i) for i in range(n32)]
            P["ptr"] = [stack.enter_context(nc.psum_tensor("pt%d_%d" % (rot["i"], i), [128, 1024], BF16)) for i in range(n16)]
            P["Bptr"] = [Buf("ptr%d" % i) for i in range(n16)]
            P["nrot"] = nrot
            rot["i"] += 1000

        def bank(lo=0, hi=None):
            hi = P["nrot"] if hi is None else hi
            i = lo + rot["i"] % (hi - lo)
            rot["i"] += 1
            return P["pb"][i], P["Bpb"][i]

        identb = sbuf(st, "identb", [128, 128], BF16)
        onesb = sbuf(st, "onesb", [128, 128], BF16)
        onesf = sbuf(st, "onesf", [1, 128], F32)
        one11 = sbuf(st, "one11", [1, 1], F32)
        epsb = sbuf(st, "epsb", [128, 1], F32)
        Gp = sbuf(st, "Gp", [128, KC], F32)
        Sp = sbuf(st, "Sp", [128, KC], F32)
        validt = sbuf(st, "validt", [128, NT], F32)
        maskbt = sbuf(st, "maskbt", [128, NT], F32)
        gq = sbuf(st, "gq", [128, 12], F32)
        gkv = sbuf(st, "gkv", [128, 4], F32)
        ggla = sbuf(st, "ggla", [128, 1], F32)
        invf = sbuf(st, "invf", [64, 1], F32)
        sgn = sbuf(st, "sgn", [64, 1], F32)
        Bconst = Buf("const")
        BGS = Buf("GS")

        with ExitStack() as s0:
            tmpf = sbuf(s0, "tmpf", [128, 128], F32)
            Btmpf = Buf("tmpf")
            S.dma("sp", lambda h: h.dma_start(out=tmpf[:], in_=ident_d[:, :]), [], [Btmpf])
            S.op("dve", lambda h: h.tensor_copy(out=identb[:], in_=tmpf[:]), [Btmpf], [Bconst])
        S.barrier()
        S.op("dve", lambda h: h.memset(onesb[:], 1.0), [], [Bconst])
        S.op("dve", lambda h: h.memset(onesf[:], 1.0), [], [Bconst])
        S.op("dve", lambda h: h.memset(one11[:], 1.0), [], [Bconst])
        S.op("dve", lambda h: h.memset(epsb[:], EPS), [], [Bconst])
        for dst, src in ((validt, valid_d), (maskbt, maskb_d), (gq, gq_d), (gkv, gkv_d),
                         (ggla, ggla_d), (invf, invf_d), (sgn, sgn_d)):
            S.dma("sp", (lambda d_, s_: (lambda h: h.dma_start(out=d_[:], in_=s_[:, :])))(dst, src),
                  [], [Bconst])

        Bsw = {}
        for name, src, dst, n in (("wkfm", wkfm, s_wkfm, 4), ("wktm", wktm, s_wktm, 4),
                                  ("wqfm", wqfm, s_wqfm, 4), ("wuq", wuq, s_wuq, 4),
                                  ("wukv", wukv, s_wukv, 4), ("wout", wout, s_wout, 4)):
            Bsw[name] = Buf("s_" + name)
            sv, dv = cast2d(src, n), cast2d(dst, n)
            R = sv.shape[0]
            step = 4096

            def castfn(h, sv=sv, dv=dv, R=R, step=step):
                out = []
                for r0 in range(0, R, step):
                    r1 = min(R, r0 + step)
                    out.append(h.dma_start(out=dv[r0:r1, :], in_=sv[r0:r1, :]))
                return out
            S.dma("pool", castfn, [], [Bsw[name]])

        with ExitStack() as s0:
            psum_alloc(s0, 8, 0, 3)
            ct = sbuf(s0, "ct", [128, KC], F32)
            scb = sbuf(s0, "scb", [128, KC], F32)
            modrow = sbuf(s0, "modrow", [1, 3 * D], F32)
            badarow = sbuf(s0, "badarow", [1, 3 * D], F32)
            grow = sbuf(s0, "grow", [1, D], F32)
            wa = [sbuf(s0, "wa%d" % i, [128, KC, 256], F32) for i in range(2)]
            gsb = sbuf(s0, "gsb", [128, 2 * KC], F32)
            Bct, Bscb, Bmod, Bbada, Bgrow = Buf("ct"), Buf("scb"), Buf("mod"), Buf("bada"), Buf("grow")
            Bwa = [Buf("wa0"), Buf("wa1")]
            S.dma("sp", lambda h: h.dma_start(out=ct[:], in_=cvec[:, :]), [], [Bct])
            S.dma("sp", lambda h: h.dma_start(out=badarow[:], in_=bada[:, :]), [], [Bbada])
            S.dma("sp", lambda h: h.dma_start(out=grow[:], in_=gpre[:, :]), [], [Bgrow])
            S.op("act", lambda h: h.activation(out=scb[:], in_=ct[:], func=AF.Silu), [Bct], [Bscb])
            for nt in range(48):
                w, Bw = wa[nt % 2], Bwa[nt % 2]
                S.dma("sp", lambda h, w=w, nt=nt: h.dma_start(out=w[:], in_=wada[nt]), [], [Bw])
                ps, Bps = bank()

                def gemv(h, ps=ps, w=w):
                    ins = None
                    for kc in range(KC):
                        ins = h.matmul(ps[0:1, 0:256], lhsT=scb[:, kc:kc + 1], rhs=w[:, kc, :],
                                       start=(kc == 0), stop=(kc == KC - 1))
                    return ins
                S.op("pe", gemv, [Bscb, Bw], [Bps])
                S.op("dve", lambda h, ps=ps, nt=nt: h.tensor_tensor(
                    out=modrow[0:1, nt * 256:(nt + 1) * 256], in0=ps[0:1, 0:256],
                    in1=badarow[0:1, nt * 256:(nt + 1) * 256], op=ALU.add), [Bps, Bbada], [Bmod])
            S.op("dve", lambda h: h.scalar_tensor_tensor(
                out=modrow[0:1, D:2 * D], in0=modrow[0:1, D:2 * D], scalar=1.0, in1=grow[0:1, :],
                op0=ALU.add, op1=ALU.mult), [Bmod, Bgrow], [Bmod])
            S.dma("sp", lambda h: h.dma_start(out=grow[:], in_=gpost[:, :]), [], [Bgrow])
            S.op("dve", lambda h: h.tensor_tensor(
                out=modrow[0:1, 2 * D:3 * D], in0=modrow[0:1, 2 * D:3 * D], in1=grow[0:1, :],
                op=ALU.mult), [Bmod, Bgrow], [Bmod])
            Bsgg = Buf("s_gg")
            S.dma("sp", lambda h: h.dma_start(out=s_gg[:, :], in_=modrow[0:1, 2 * D:3 * D]), [Bmod], [Bsgg])
            ps, Bps = bank()

            def rows2p(h, ps=ps):
                ins = None
                for j, off in enumerate((D, 0)):
                    for kc in range(KC):
                        ins = h.matmul(ps[:, j * KC + kc:j * KC + kc + 1],
                                       lhsT=modrow[0:1, off + kc * 128:off + (kc + 1) * 128],
                                       rhs=one11[0:1, 0:1], start=True, stop=True)
                return ins
            S.op("pe", rows2p, [Bmod, Bconst], [Bps])
            S.op("dve", lambda h, ps=ps: h.tensor_copy(out=Gp[:], in_=ps[:, 0:KC]), [Bps], [BGS])
            S.op("dve", lambda h, ps=ps: h.tensor_copy(out=Sp[:], in_=ps[:, KC:2 * KC]), [Bps], [BGS])

        S.barrier()
        if stop == 0:
            S.wait_all("sp", S.dmabufs)
            return nc
        def make_hT(xrow0, xs, Bxs, xn, Bxn, ssq, Bssq, hT, BhT):
            for tt in range(4):
                x_, Bx_ = xs[tt % 2], Bxs[tt % 2]
                n_, Bn_ = xn[tt % 2], Bxn[tt % 2]
                r0 = xrow0 + tt * 128
                S.dma("sp", lambda h, x_=x_, r0=r0: h.dma_start(out=x_[:], in_=xl[r0:r0 + 128, :]), [], [Bx_])
                S.op("act", lambda h, x_=x_, n_=n_: h.activation(
                    out=n_[:], in_=x_[:], func=AF.Square, accum_out=ssq[:, 0:1]), [Bx_], [Bn_, Bssq])
                S.op("act", lambda h: h.activation(out=ssq[:, 0:1], in_=ssq[:, 0:1], func=AF.Sqrt,
                                                   scale=1.0 / D, bias=epsb[:, 0:1]), [Bssq, Bconst], [Bssq])
                S.op("dve", lambda h: h.reciprocal(out=ssq[:, 0:1], in_=ssq[:, 0:1]), [Bssq], [Bssq])
                S.op("act", lambda h, x_=x_, n_=n_: h.activation(
                    out=n_[:], in_=x_[:], func=AF.Copy, scale=ssq[:, 0:1]), [Bx_, Bssq], [Bn_])
                for g in range(8):
                    half = g % 2
                    pt = P["ptr"][half][:, 0:512]

                    def tr(h, pt=pt, n_=n_, g=g):
                        ins = None
                        for j in range(4):
                            kc = g * 4 + j
                            ins = h.transpose(out=pt[:, j * 128:(j + 1) * 128],
                                              in_=n_[:, kc * 128:(kc + 1) * 128], identity=identb[:])
                        return ins
                    S.op("pe", tr, [Bn_, Bconst], [P["Bptr"][half]])
                    for j in range(4):
                        kc = g * 4 + j
                        S.op("act", lambda h, pt=pt, j=j, kc=kc, tt=tt: h.activation(
                            out=hT[:, kc, tt * 128:(tt + 1) * 128], in_=pt[:, j * 128:(j + 1) * 128],
                            func=AF.Identity, scale=Gp[:, kc:kc + 1], bias=Sp[:, kc:kc + 1]),
                            [P["Bptr"][half], BGS], [BhT])

        def rope_tables(s1, tok0, n, pre):
            posi = sbuf(s1, pre + "posi", [64, n], I32)
            ang = sbuf(s1, pre + "ang", [64, n], F32)
            kf = sbuf(s1, pre + "kf", [64, n], F32)
            cosT = sbuf(s1, pre + "cosT", [64, n], F32)
            sinT = sbuf(s1, pre + "sinT", [64, n], F32)
            Bp, Ba, Bk, Bc, Bs = (Buf(pre + x) for x in ("posi", "ang", "kf", "cos", "sin"))

            def emit(tok0):
                S.dma("sp", lambda h: h.dma_start(out=posi[:], in_=pos64[:, tok0:tok0 + n]), [], [Bp])
                for shift, dst, Bd, use_sgn in ((0.0, sinT, Bs, True), (float(np.pi / 2), cosT, Bc, False)):
                    S.op("dve", lambda h: h.tensor_copy(out=ang[:], in_=posi[:]), [Bp], [Ba])
                    S.op("dve", lambda h, shift=shift: h.tensor_scalar(
                        out=ang[:], in0=ang[:], scalar1=invf[:, 0:1], scalar2=shift,
                        op0=ALU.mult, op1=ALU.add), [Ba, Bconst], [Ba])
                    S.op("dve", lambda h: h.tensor_scalar(out=kf[:], in0=ang[:], scalar1=float(1.0 / TWO_PI),
                                                          scalar2=None, op0=ALU.mult), [Ba], [Bk])
                    S.op("dve", lambda h: h.tensor_copy(out=posi[:], in_=kf[:]), [Bk], [Bp])
                    S.op("dve", lambda h: h.tensor_copy(out=kf[:], in_=posi[:]), [Bp], [Bk])
                    S.op("dve", lambda h: h.scalar_tensor_tensor(out=ang[:], in0=kf[:], scalar=-C1, in1=ang[:],
                                                                 op0=ALU.mult, op1=ALU.add), [Bk, Ba], [Ba])
                    S.op("dve", lambda h: h.scalar_tensor_tensor(out=ang[:], in0=kf[:], scalar=-C2, in1=ang[:],
                                                                 op0=ALU.mult, op1=ALU.add), [Bk, Ba], [Ba])
                    S.op("dve", lambda h: h.tensor_scalar(out=kf[:], in0=ang[:], scalar1=float(np.pi),
                                                          scalar2=-TWO_PI, op0=ALU.is_gt, op1=ALU.mult), [Ba], [Bk])
                    S.op("dve", lambda h: h.tensor_tensor(out=ang[:], in0=ang[:], in1=kf[:], op=ALU.add), [Ba, Bk], [Ba])
                    S.op("dve", lambda h: h.tensor_scalar(out=kf[:], in0=ang[:], scalar1=float(-np.pi),
                                                          scalar2=TWO_PI, op0=ALU.is_lt, op1=ALU.mult), [Ba], [Bk])
                    S.op("dve", lambda h: h.tensor_tensor(out=ang[:], in0=ang[:], in1=kf[:], op=ALU.add), [Ba, Bk], [Ba])
                    S.op("dve", lambda h: h.tensor_scalar(out=ang[:], in0=ang[:], scalar1=3.1415925, scalar2=-3.1415925,
                                                          op0=ALU.min, op1=ALU.max), [Ba], [Ba])
                    if use_sgn:
                        S.op("act", lambda h, dst=dst: h.activation(out=dst[:], in_=ang[:], func=AF.Sin,
                                                                    scale=sgn[:, 0:1]), [Ba, Bconst], [Bd])
                    else:
                        S.op("act", lambda h, dst=dst: h.activation(out=dst[:], in_=ang[:], func=AF.Sin), [Ba], [Bd])
                    if use_sgn:
                        S.dma("sp", lambda h: h.dma_start(out=posi[:], in_=pos64[:, tok0:tok0 + n]), [], [Bp])
            return emit, cosT, sinT, Bc, Bs

        Blat = [Buf("lat%d" % s) for s in range(NS)]
        latgrp = SemGroup("latgrp")
        for b in Blat:
            b.grp = latgrp
        BSsn = Buf("Ssn")
        with ExitStack() as s1:
            psum_alloc(s1, 6, 2, 2)
            pb, Bpb = P["pb"], P["Bpb"]
            xs = [sbuf(s1, "xs%d" % i, [128, D], F32) for i in range(2)]
            xn = [sbuf(s1, "xn%d" % i, [128, D], BF16) for i in range(2)]
            Bxs, Bxn = [Buf("xs0"), Buf("xs1")], [Buf("xn0"), Buf("xn1")]
            ssq = sbuf(s1, "ssq", [128, 1], F32)
            Bssq = Buf("ssq")
            hT = sbuf(s1, "hT", [128, KC, 512], BF16)
            BhT = Buf("hT")
            wfm = [sbuf(s1, "wfm%d" % i, [128, KC, 128], BF16) for i in range(2)]
            Bwfm = [Buf("wfm0"), Buf("wfm1")]
            wtm = [sbuf(s1, "wtm%d" % i, [128, KC, 256], BF16) for i in range(2)]
            Bwtm = [Buf("wtm0"), Buf("wtm1")]
            walr = sbuf(s1, "walr", [128, KC, 16], BF16)
            Bwalr = Buf("walr")
            ckvf = sbuf(s1, "ckvf", [128, 4, 512], F32)
            sqb = sbuf(s1, "sqb", [128, 512], BF16)
            rstdb = sbuf(s1, "rstdb", [128, 512], F32)
            latbf = sbuf(s1, "latbf", [128, 4, 512], BF16)
            krf = sbuf(s1, "krf", [64, 512], F32)
            krs = sbuf(s1, "krs", [64, 512], F32)
            krb = sbuf(s1, "krb", [64, 512], BF16)
            Bckvf, Bsqb, Brstdb, Blatbf, Bkrf, Bkrs, Bkrb = (Buf(n) for n in
                                                             ("ckvf", "sqb", "rstdb", "latbf", "krf", "krs", "krb"))
            ktm = sbuf(s1, "ktm", [128, 4, 1024], BF16)
            vtm = sbuf(s1, "vtm", [128, 4, 2048], BF16)
            Bktm, Bvtm = Buf("ktm"), Buf("vtm")
            alrT = sbuf(s1, "alrT", [32, 512], F32)
            wup = sbuf(s1, "wup", [32, 1024], F32)
            lt = sbuf(s1, "lt", [128, 1024], F32)
            kdec = sbuf(s1, "kdec", [128, 1024], BF16)
            Sst = sbuf(s1, "Sst", [128, 8, 128], F32)
            Sbf = sbuf(s1, "Sbf", [128, 8, 128], BF16)
            dec = sbuf(s1, "dec", [128, 16], F32)
            tri = sbuf(s1, "tri", [128, 128], F32)
            cind = sbuf(s1, "cind", [128, 2], F32)
            BalrT, Bwup, Blt, Bkdec, BSst, BSbf, Bdec, Bgc = (Buf(n) for n in
                                                              ("alrT", "wup", "lt", "kdec", "Sst", "Sbf", "dec", "gc"))
            rope_emit, cosT, sinT, Bcos, Bsin = rope_tables(s1, 0, 512, "r1")

            S.dma("pool", lambda h: h.dma_start(out=walr[:], in_=walr_d[:, :, :]), [], [Bwalr])
            S.op("dve", lambda h: h.memset(alrT[:], 1.0), [], [BalrT])
            S.dma("sp", lambda h: h.dma_start(out=wup[0:17, :], in_=wup_d[:, :]), [], [Bwup])
            S.dma("sp", lambda h: h.dma_start(out=tri[:], in_=tri_d[:, :]), [], [Bgc])
            S.dma("sp", lambda h: h.dma_start(out=cind[:], in_=cind_d[:, :]), [], [Bgc])
            S.op("dve", lambda h: h.memset(Sst[:], 0.0), [], [BSst])

            wi = {"fm": 0, "tm": 0}

            def load_fm(src_ap, Bsrc):
                i = wi["fm"] % 2
                wi["fm"] += 1
                S.dma("sp", lambda h: h.dma_start(out=wfm[i][:], in_=src_ap), [Bsrc], [Bwfm[i]])
                return wfm[i], Bwfm[i]

            def load_tm(src_ap, Bsrc):
                i = wi["tm"] % 2
                wi["tm"] += 1
                S.dma("sp", lambda h: h.dma_start(out=wtm[i][:], in_=src_ap), [Bsrc], [Bwtm[i]])
                return wtm[i], Bwtm[i]

            for s in range(NS):
                t0 = s * 512
                make_hT(t0, xs, Bxs, xn, Bxn, ssq, Bssq, hT, BhT)
                if stop == 10:
                    S.wait_all("sp", S.dmabufs)
                    return nc
                rope_emit(t0)
                if stop == 11:
                    S.wait_all("sp", S.dmabufs)
                    return nc
                psq, Bpsq = pb[2], Bpb[2]
                for ctile in range(4):
                    w, Bw = load_fm(s_wkfm[ctile], Bsw["wkfm"])
                    ps, Bps = bank()

                    def mm(h, ps=ps, w=w):
                        ins = None
                        for kc in range(KC):
                            ins = h.matmul(ps[:], lhsT=w[:, kc, :], rhs=hT[:, kc, :],
                                           start=(kc == 0), stop=(kc == KC - 1))
                        return ins
                    S.op("pe", mm, [Bw, BhT], [Bps])
                    S.op("act", lambda h, ps=ps, ctile=ctile: h.activation(
                        out=ckvf[:, ctile, :], in_=ps[:], func=AF.Copy), [Bps], [Bckvf])
                    S.op("act", lambda h, ps=ps: h.activation(out=sqb[:], in_=ps[:], func=AF.Square), [Bps], [Bsqb])
                    S.op("pe", lambda h, ctile=ctile: h.matmul(psq[:], lhsT=onesb[:], rhs=sqb[:],
                                                               start=(ctile == 0), stop=(ctile == 3),
                                                               skip_group_check=True),
                         [Bsqb, Bconst], [Bpsq])
                S.op("act", lambda h: h.activation(out=rstdb[:], in_=psq[:], func=AF.Sqrt, scale=1.0 / 512,
                                                   bias=epsb[:, 0:1]), [Bpsq, Bconst], [Brstdb])
                S.op("dve", lambda h: h.reciprocal(out=rstdb[:], in_=rstdb[:]), [Brstdb], [Brstdb])
                for ctile in range(4):
                    S.op("dve", lambda h, ctile=ctile: h.scalar_tensor_tensor(
                        out=latbf[:, ctile, :], in0=ckvf[:, ctile, :], scalar=gkv[:, ctile:ctile + 1],
                        in1=rstdb[:], op0=ALU.mult, op1=ALU.mult), [Bckvf, Brstdb, Bconst], [Blatbf])
                S.dma("sp", lambda h, t0=t0: h.dma_start(
                    out=s_lat[0:4, :, t0:t0 + 512].rearrange("c p t -> p c t"), in_=latbf[:]),
                    [Blatbf], [Blat[s]])
                if stop == 12:
                    S.wait_all("sp", S.dmabufs)
                    return nc
                w, Bw = load_fm(s_wkfm[4], Bsw["wkfm"])
                for half, dst, Bd in ((0, krf, Bkrf), (1, krs, Bkrs)):
                    ps, Bps = bank()

                    def mm(h, ps=ps, w=w, half=half):
                        ins = None
                        for kc in range(KC):
                            ins = h.matmul(ps[0:64, :], lhsT=w[:, kc, half * 64:(half + 1) * 64], rhs=hT[:, kc, :],
                                           start=(kc == 0), stop=(kc == KC - 1))
                        return ins
                    S.op("pe", mm, [Bw, BhT], [Bps])
                    tab, Btab = (cosT, Bcos) if half == 0 else (sinT, Bsin)
                    S.op("dve", lambda h, ps=ps, dst=dst, tab=tab: h.tensor_tensor(
                        out=dst[:], in0=ps[0:64, :], in1=tab[:], op=ALU.mult), [Bps, Btab], [Bd])
                S.op("dve", lambda h: h.tensor_tensor(out=krb[:], in0=krf[:], in1=krs[:], op=ALU.add),
                     [Bkrf, Bkrs], [Bkrb])
                S.dma("sp", lambda h, t0=t0: h.dma_start(out=s_lat[4, 0:64, t0:t0 + 512], in_=krb[:]),
                      [Bkrb], [Blat[s]])
                if stop == 13:
                    S.wait_all("sp", S.dmabufs)
                    return nc
                ps, Bps = bank()

                def mm(h, ps=ps):
                    ins = None
                    for kc in range(KC):
                        ins = h.matmul(ps[0:16, :], lhsT=walr[:, kc, :], rhs=hT[:, kc, :],
                                       start=(kc == 0), stop=(kc == KC - 1))
                    return ins
                S.op("pe", mm, [Bwalr, BhT], [Bps])
                S.op("act", lambda h, ps=ps: h.activation(out=alrT[0:16, :], in_=ps[0:16, :], func=AF.Copy),
                     [Bps], [BalrT])
                if stop == 14:
                    S.wait_all("sp", S.dmabufs)
                    return nc
                for ctile in range(12):
                    w, Bw = load_tm(s_wktm[ctile], Bsw["wktm"])
                    for tt in range(4):
                        ps, Bps = bank()

                        def mm(h, ps=ps, w=w, tt=tt):
                            ins = None
                            for kc in range(KC):
                                ins = h.matmul(ps[:, 0:256], lhsT=hT[:, kc, tt * 128:(tt + 1) * 128], rhs=w[:, kc, :],
                                               start=(kc == 0), stop=(kc == KC - 1))
                            return ins
                        S.op("pe", mm, [Bw, BhT], [Bps])
                        if ctile < 4:
                            S.op("act", lambda h, ps=ps, tt=tt, ctile=ctile: h.activation(
                                out=ktm[:, tt, ctile * 256:(ctile + 1) * 256], in_=ps[:, 0:256], func=AF.Copy),
                                [Bps], [Bktm])
                        else:
                            c2 = ctile - 4
                            tg = s * 4 + tt
                            S.op("act", lambda h, ps=ps, tt=tt, c2=c2, tg=tg: h.activation(
                                out=vtm[:, tt, c2 * 256:(c2 + 1) * 256], in_=ps[:, 0:256], func=AF.Copy,
                                scale=validt[:, tg:tg + 1]), [Bps, Bconst], [Bvtm])
                if stop == 15:
                    S.wait_all("sp", S.dmabufs)
                    return nc
                for tt in range(4):
                    for hf in range(2):
                        ps, Bps = pb[3], Bpb[3]
                        S.op("pe", lambda h, ps=ps, tt=tt, hf=hf: h.matmul(
                            ps[:], lhsT=alrT[0:17, tt * 128:(tt + 1) * 128], rhs=wup[0:17, hf * 512:(hf + 1) * 512],
                            start=True, stop=True), [BalrT, Bwup], [Bps])
                        S.op("act", lambda h, ps=ps, hf=hf: h.activation(
                            out=lt[:, hf * 512:(hf + 1) * 512], in_=ps[:], func=AF.Exp, scale=-1.0), [Bps], [Blt])
                    S.op("act", lambda h: h.activation(out=lt[:], in_=lt[:], func=AF.Ln, bias=1.0), [Blt], [Blt])
                    for hf in range(2):
                        ps, Bps = pb[3], Bpb[3]
                        S.op("pe", lambda h, ps=ps, hf=hf: h.matmul(
                            ps[:], lhsT=tri[:], rhs=lt[:, hf * 512:(hf + 1) * 512], start=True, stop=True),
                            [Bgc, Blt], [Bps])
                        S.op("act", lambda h, ps=ps, hf=hf: h.activation(
                            out=kdec[:, hf * 512:(hf + 1) * 512], in_=ps[:], func=AF.Exp), [Bps], [Bkdec])
                    S.op("dve", lambda h, tt=tt: h.tensor_tensor(out=kdec[:], in0=kdec[:], in1=ktm[:, tt, :], op=ALU.mult),
                         [Bkdec, Bktm], [Bkdec])
                    ps, Bps = pb[3], Bpb[3]

                    def mmlt(h, ps=ps):
                        ins = None
                        for pr in range(8):
                            ins = h.matmul(ps[:, pr * 2:pr * 2 + 2], lhsT=lt[:, pr * 128:(pr + 1) * 128], rhs=cind[:],
                                           start=True, stop=True)
                        return ins
                    S.op("pe", mmlt, [Blt, Bgc], [Bps])
                    S.op("act", lambda h, ps=ps: h.activation(out=dec[:], in_=ps[:, 0:16], func=AF.Exp), [Bps], [Bdec])
                    for c in range(2):
                        n = (s * 4 + tt) * 2 + c
                        r0 = c * 64
                        kvp = [pb[4], pb[5]]

                        def mmkv(h, r0=r0, tt=tt):
                            ins = None
                            for hd in range(NH):
                                pr, od = hd // 2, hd % 2
                                ins = h.matmul(kvp[pr // 4][od * 64:(od + 1) * 64, (pr % 4) * 128:(pr % 4 + 1) * 128],
                                               lhsT=kdec[r0:r0 + 64, hd * 64:(hd + 1) * 64],
                                               rhs=vtm[r0:r0 + 64, tt, hd * 128:(hd + 1) * 128],
                                               start=True, stop=True, tile_position=(r0, od * 64))
                            return ins
                        S.op("pe", mmkv, [Bkdec, Bvtm], [Bpb[4], Bpb[5]])
                        for pr in range(8):
                            S.op("dve", lambda h, pr=pr, c=c: h.scalar_tensor_tensor(
                                out=Sst[:, pr, :], in0=Sst[:, pr, :], scalar=dec[:, pr * 2 + c:pr * 2 + c + 1],
                                in1=kvp[pr // 4][:, (pr % 4) * 128:(pr % 4 + 1) * 128], op0=ALU.mult, op1=ALU.add),
                                [BSst, Bdec, Bpb[4 + pr // 4]], [BSst])
                        if n >= NCH - 16:
                            S.op("act", lambda h: h.activation(out=Sbf[:], in_=Sst[:], func=AF.Copy), [BSst], [BSbf])
                            S.dma("sp", lambda h, n=n: h.dma_start(
                                out=s_Ssn[n - (NCH - 16)], in_=Sbf[:].rearrange("p a b -> p (a b)")), [BSbf], [BSsn])

        S.barrier()
        if stop == 1:
            S.wait_all("sp", S.dmabufs)
            return nc
        Bcqn, Bmg = Buf("s_cqn"), Buf("s_mg")
        BoT = [Buf("s_oT%d" % i) for i in range(KC)]
        with ExitStack() as s1:
            psum_alloc(s1, 6, 2, 2)
            pb, Bpb = P["pb"], P["Bpb"]
            xs = [sbuf(s1, "bxs%d" % i, [128, D], F32) for i in range(2)]
            xn = [sbuf(s1, "bxn%d" % i, [128, D], BF16) for i in range(2)]
            Bxs, Bxn = [Buf("bxs0"), Buf("bxs1")], [Buf("bxn0"), Buf("bxn1")]
            ssq = sbuf(s1, "bssq", [128, 1], F32)
            Bssq = Buf("bssq")
            hT = sbuf(s1, "bhT", [128, KC, 512], BF16)
            BhT = Buf("bhT")
            wfm = [sbuf(s1, "bwfm%d" % i, [128, KC, 128], BF16) for i in range(3)]
            Bwfm = [Buf("bwfm%d" % i) for i in range(3)]
            qT = sbuf(s1, "qT", [128, 8, 512], BF16)
            BqT = Buf("qT")
            Ssn = sbuf(s1, "Ssn", [128, 8, 8 * 128], BF16)
            BSs = Buf("Ssn_sb")
            osq = sbuf(s1, "osq", [128, 512], BF16)
            orst = sbuf(s1, "orst", [128, 512], F32)
            onT = sbuf(s1, "onT", [128, NH, 512], BF16)
            og = sbuf(s1, "og", [128, 512], BF16)
            cqb = sbuf(s1, "cqb", [128, 12, 512], BF16)
            cqf = sbuf(s1, "cqf", [128, 512], F32)
            mgb = sbuf(s1, "mgb", [128, 512], BF16)
            Bosq, Borst, BonT, Bog, Bcqb, Bcqf, Bmgb = (Buf(n) for n in
                                                        ("osq", "orst", "onT", "og", "cqb", "cqf", "mgb"))
            wj = {"i": 0}

            def load_q(idx):
                i = wj["i"] % 3
                wj["i"] += 1
                S.dma("sp", lambda h: h.dma_start(out=wfm[i][:], in_=s_wqfm[idx]), [Bsw["wqfm"]], [Bwfm[i]])
                return wfm[i], Bwfm[i]

            def fm_mm(w, Bw):
                ps, Bps = bank()

                def mm(h):
                    ins = None
                    for kc in range(KC):
                        ins = h.matmul(ps[:], lhsT=w[:, kc, :], rhs=hT[:, kc, :], start=(kc == 0), stop=(kc == KC - 1))
                    return ins
                S.op("pe", mm, [Bw, BhT], [Bps])
                return ps, Bps

            for so in range(2):
                t0 = OWN0 + so * 512
                o0 = so * 512
                make_hT(t0, xs, Bxs, xn, Bxn, ssq, Bssq, hT, BhT)
                for pr in range(8):
                    w, Bw = load_q(pr)
                    ps, Bps = fm_mm(w, Bw)
                    S.op("act", lambda h, ps=ps, pr=pr: h.activation(out=qT[:, pr, :], in_=ps[:], func=AF.Copy,
                                                                     scale=0.125), [Bps], [BqT])
                S.dma("sp", lambda h, so=so: h.dma_start(
                    out=Ssn[:], in_=s_Ssn[so * 8:(so + 1) * 8].rearrange("n p f -> p n f")), [BSsn], [BSs])
                for hd in range(NH):
                    pr, od = hd // 2, hd % 2
                    ps, Bps = bank()

                    def mmo(h, ps=ps, pr=pr, od=od):
                        ins = None
                        for cn in range(8):
                            ins = h.matmul(ps[:, cn * 64:(cn + 1) * 64],
                                           lhsT=Ssn[od * 64:(od + 1) * 64, cn, pr * 128:(pr + 1) * 128],
                                           rhs=qT[od * 64:(od + 1) * 64, pr, cn * 64:(cn + 1) * 64],
                                           start=True, stop=True, tile_position=(od * 64, 0))
                        return ins
                    S.op("pe", mmo, [BSs, BqT], [Bps])
                    S.op("act", lambda h, ps=ps: h.activation(out=osq[:], in_=ps[:], func=AF.Square), [Bps], [Bosq])
                    S.op("pe", lambda h: h.matmul(pb[2][:], lhsT=onesb[:], rhs=osq[:], start=True, stop=True),
                         [Bosq, Bconst], [Bpb[2]])
                    S.op("act", lambda h: h.activation(out=orst[:], in_=pb[2][:], func=AF.Sqrt, scale=1.0 / 128,
                                                       bias=epsb[:, 0:1]), [Bpb[2], Bconst], [Borst])
                    S.op("dve", lambda h: h.reciprocal(out=orst[:], in_=orst[:]), [Borst], [Borst])
                    S.op("dve", lambda h, ps=ps, hd=hd: h.scalar_tensor_tensor(
                        out=onT[:, hd, :], in0=ps[:], scalar=ggla[:, 0:1], in1=orst[:], op0=ALU.mult, op1=ALU.mult),
                        [Bps, Borst, Bconst], [BonT])
                for hd in range(NH):
                    w, Bw = load_q(8 + hd)
                    ps, Bps = fm_mm(w, Bw)
                    S.op("act", lambda h, ps=ps: h.activation(out=og[:], in_=ps[:], func=AF.Silu), [Bps], [Bog])
                    S.op("dve", lambda h, hd=hd: h.tensor_tensor(out=onT[:, hd, :], in0=onT[:, hd, :], in1=og[:],
                                                                  op=ALU.mult), [BonT, Bog], [BonT])
                    S.dma("sp", lambda h, hd=hd, o0=o0: h.dma_start(out=s_oT[hd, :, o0:o0 + 512], in_=onT[:, hd, :]),
                          [BonT], [BoT[hd]])
                for ctile in range(12):
                    w, Bw = load_q(24 + ctile)
                    ps, Bps = fm_mm(w, Bw)
                    S.op("act", lambda h, ps=ps, ctile=ctile: h.activation(out=cqb[:, ctile, :], in_=ps[:], func=AF.Copy),
                         [Bps], [Bcqb])
                    S.op("act", lambda h, ps=ps: h.activation(out=osq[:], in_=ps[:], func=AF.Square), [Bps], [Bosq])
                    S.op("pe", lambda h, ctile=ctile: h.matmul(pb[2][:], lhsT=onesb[:], rhs=osq[:],
                                                               start=(ctile == 0), stop=(ctile == 11),
                                                               skip_group_check=True),
                         [Bosq, Bconst], [Bpb[2]])
                S.op("act", lambda h: h.activation(out=orst[:], in_=pb[2][:], func=AF.Sqrt, scale=1.0 / 1536,
                                                   bias=epsb[:, 0:1]), [Bpb[2], Bconst], [Borst])
                S.op("dve", lambda h: h.reciprocal(out=orst[:], in_=orst[:]), [Borst], [Borst])
                for ctile in range(12):
                    S.op("dve", lambda h, ctile=ctile: h.scalar_tensor_tensor(
                        out=cqb[:, ctile, :], in0=cqb[:, ctile, :], scalar=gq[:, ctile:ctile + 1], in1=orst[:],
                        op0=ALU.mult, op1=ALU.mult), [Bcqb, Borst, Bconst], [Bcqb])
                S.dma("sp", lambda h, o0=o0: h.dma_start(
                    out=s_cqn[:, :, o0:o0 + 512].rearrange("c p t -> p c t"), in_=cqb[:]), [Bcqb], [Bcqn])
                for hd in range(NH):
                    w, Bw = load_q(36 + hd)
                    ps, Bps = fm_mm(w, Bw)
                    S.op("act", lambda h, ps=ps: h.activation(out=mgb[:], in_=ps[:], func=AF.Silu), [Bps], [Bmgb])
                    S.dma("sp", lambda h, hd=hd, o0=o0: h.dma_start(out=s_mg[hd, :, o0:o0 + 512], in_=mgb[:]),
                          [Bmgb], [Bmg])

        S.barrier()
        if stop == 2:
            S.wait_all("sp", S.dmabufs)
            return nc
        with ExitStack() as s2:
            psum_alloc(s2, 8, 0, 3)
            pb, Bpb = P["pb"], P["Bpb"]
            latT = sbuf(s2, "latT", [128, 4, L], BF16)
            krT = sbuf(s2, "krT", [64, L], BF16)
            cqn = sbuf(s2, "cqn", [128, 12, BLK], BF16)
            Blatsb, Bkrsb, Bcqsb = Buf("latT"), Buf("krT"), Buf("cqn")
            S.dma("sp", lambda h: [h.dma_start(out=latT[:, c, :], in_=s_lat[c]) for c in range(4)], Blat, [Blatsb])
            S.dma("sp", lambda h: h.dma_start(out=krT[:], in_=s_lat[4, 0:64, :]), Blat, [Bkrsb])
            S.dma("sp", lambda h: h.dma_start(out=cqn[:], in_=s_cqn.rearrange("c p t -> p c t")), [Bcqn], [Bcqsb])
            rope_emit, cosT, sinT, Bcos, Bsin = rope_tables(s2, OWN0, BLK, "r2")
            rope_emit(OWN0)
            wq = [sbuf(s2, "wq%d" % i, [128, 12, 256], BF16) for i in range(2)]
            wkv = [sbuf(s2, "wkv%d" % i, [128, 4, 256], BF16) for i in range(2)]
            Bwq, Bwkv = [Buf("wq0"), Buf("wq1")], [Buf("wkv0"), Buf("wkv1")]
            KT = sbuf(s2, "KT", [128, L], BF16)
            Vt = sbuf(s2, "Vt", [128, NT, 128], BF16)
            qn = sbuf(s2, "qn", [128, BLK], BF16)
            qr = sbuf(s2, "qr", [64, BLK], BF16)
            qa = sbuf(s2, "qa", [64, 512], F32)
            qb_ = sbuf(s2, "qb", [64, 512], F32)
            PT = [sbuf(s2, "PT%d" % i, [128, 512], BF16) for i in range(3)]
            rden = sbuf(s2, "rden", [128, 512], F32)
            mgs = sbuf(s2, "mgs", [128, 512], BF16)
            ob = sbuf(s2, "ob", [128, 512], BF16)
            BKT, BVt, Bqn, Bqr, Bqa, Bqb, Brden, Bmgs, Bob = (Buf(n) for n in
                                                              ("KT", "Vt", "qn", "qr", "qa", "qb", "rden", "mgs", "ob"))
            BPT = [Buf("PT%d" % i) for i in range(3)]
            pti = {"i": 0}
            SCALE = float(192 ** -0.5)
            for hd in range(NH):
                i = hd % 2
                S.dma("sp", lambda h, i=i, hd=hd: h.dma_start(out=wq[i][:], in_=s_wuq[hd]), [Bsw["wuq"]], [Bwq[i]])
                S.dma("sp", lambda h, i=i, hd=hd: h.dma_start(out=wkv[i][:], in_=s_wukv[hd]), [Bsw["wukv"]], [Bwkv[i]])
                for tb in range(L // 512):
                    ps, Bps = bank()

                    def mm(h, ps=ps, tb=tb, i=i):
                        ins = None
                        for kc in range(4):
                            ins = h.matmul(ps[:], lhsT=wkv[i][:, kc, 0:128], rhs=latT[:, kc, tb * 512:(tb + 1) * 512],
                                           start=(kc == 0), stop=(kc == 3))
                        return ins
                    S.op("pe", mm, [Bwkv[i], Blatsb], [Bps])
                    S.op("dve", lambda h, ps=ps, tb=tb: h.tensor_copy(out=KT[:, tb * 512:(tb + 1) * 512], in_=ps[:]),
                         [Bps], [BKT])
                for tb in range(NT // 4):
                    ps, Bps = bank()

                    def mm(h, ps=ps, tb=tb, i=i):
                        ins = None
                        for j in range(4):
                            tk = tb * 4 + j
                            for kc in range(4):
                                ins = h.matmul(ps[:, j * 128:(j + 1) * 128], lhsT=latT[:, kc, tk * 128:(tk + 1) * 128],
                                               rhs=wkv[i][:, kc, 128:256], start=(kc == 0), stop=(kc == 3))
                        return ins
                    S.op("pe", mm, [Bwkv[i], Blatsb], [Bps])
                    S.op("act", lambda h, ps=ps, tb=tb: h.activation(
                        out=Vt[:, tb * 4:(tb + 1) * 4, :].rearrange("p a b -> p (a b)"), in_=ps[:], func=AF.Copy),
                        [Bps], [BVt])
                for qt in range(2):
                    ps, Bps = bank()

                    def mm(h, ps=ps, qt=qt, i=i):
                        ins = None
                        for kc in range(12):
                            ins = h.matmul(ps[:], lhsT=wq[i][:, kc, 0:128], rhs=cqn[:, kc, qt * 512:(qt + 1) * 512],
                                           start=(kc == 0), stop=(kc == 11))
                        return ins
                    S.op("pe", mm, [Bwq[i], Bcqsb], [Bps])
                    S.op("act", lambda h, ps=ps, qt=qt: h.activation(out=qn[:, qt * 512:(qt + 1) * 512], in_=ps[:],
                                                                     func=AF.Copy), [Bps], [Bqn])
                    for half, dst, Bd, tab, Btab in ((0, qa, Bqa, cosT, Bcos), (1, qb_, Bqb, sinT, Bsin)):
                        ps, Bps = bank()

                        def mm(h, ps=ps, qt=qt, i=i, half=half):
                            ins = None
                            for kc in range(12):
                                ins = h.matmul(ps[0:64, :], lhsT=wq[i][:, kc, 128 + half * 64:192 + half * 64],
                                               rhs=cqn[:, kc, qt * 512:(qt + 1) * 512], start=(kc == 0), stop=(kc == 11))
                            return ins
                        S.op("pe", mm, [Bwq[i], Bcqsb], [Bps])
                        S.op("dve", lambda h, ps=ps, dst=dst, tab=tab, qt=qt: h.tensor_tensor(
                            out=dst[:], in0=ps[0:64, :], in1=tab[:, qt * 512:(qt + 1) * 512], op=ALU.mult),
                            [Bps, Btab], [Bd])
                    S.op("dve", lambda h, qt=qt: h.tensor_tensor(out=qr[:, qt * 512:(qt + 1) * 512], in0=qa[:], in1=qb_[:],
                                                                  op=ALU.add), [Bqa, Bqb], [Bqr])
                for qt in range(2):
                    q0 = qt * 512
                    nfull = (OWN0 + q0) // 128
                    po, Bpo = pb[5], Bpb[5]
                    pd, Bpd = pb[6], Bpb[6]
                    blocks = [(kb, 0) for kb in range(nfull)] + [(nfull + d, d * 128) for d in range(4)]
                    sc = {}

                    def scores(bi):
                        kb, c0 = blocks[bi]
                        k0 = kb * 128
                        ps, Bps = bank(0, 3)

                        def mm(h, ps=ps, k0=k0, c0=c0, q0=q0):
                            h.matmul(ps[:, c0:512], lhsT=KT[:, k0:k0 + 128], rhs=qn[:, q0 + c0:q0 + 512],
                                     start=True, stop=False)
                            return h.matmul(ps[:, c0:512], lhsT=krT[:, k0:k0 + 128], rhs=qr[:, q0 + c0:q0 + 512],
                                            start=False, stop=True)
                        S.op("pe", mm, [BKT, Bkrsb, Bqn, Bqr], [Bps])
                        sc[bi] = (ps, Bps)
                    for bi in range(min(2, len(blocks))):
                        scores(bi)
                    for bi, (kb, c0) in enumerate(blocks):
                        ps, Bps = sc.pop(bi)
                        pi = pti["i"] % 3
                        pti["i"] += 1
                        P_, BP_ = PT[pi], BPT[pi]
                        S.op("act", lambda h, ps=ps, P_=P_, c0=c0, kb=kb: h.activation(
                            out=P_[:, c0:512], in_=ps[:, c0:512], func=AF.Exp, scale=SCALE,
                            bias=maskbt[:, kb:kb + 1]), [Bps, Bconst], [BP_])
                        if kb >= nfull:
                            S.op("dve", lambda h, P_=P_, c0=c0: h.memset(P_[64:128, c0:c0 + 64], 0.0), [], [BP_])
                        if bi + 2 < len(blocks):
                            scores(bi + 2)
                        first, last = (bi == 0), (bi == len(blocks) - 1)

                        def mmpv(h, P_=P_, kb=kb, c0=c0, first=first, last=last):
                            h.matmul(po[:, c0:512], lhsT=Vt[:, kb, :], rhs=P_[:, c0:512], start=first, stop=last,
                                     skip_group_check=True)
                            return h.matmul(pd[:, c0:512], lhsT=onesb[:], rhs=P_[:, c0:512], start=first, stop=last,
                                            skip_group_check=True)
                        S.op("pe", mmpv, [BVt, BP_, Bconst], [Bpo, Bpd])
                    S.op("dve", lambda h: h.reciprocal(out=rden[:], in_=pd[:]), [Bpd], [Brden])
                    S.dma("sp", lambda h, hd=hd, q0=q0: h.dma_start(out=mgs[:], in_=s_mg[hd, :, q0:q0 + 512]), [Bmg], [Bmgs])
                    S.op("dve", lambda h: h.tensor_tensor(out=rden[:], in0=rden[:], in1=mgs[:], op=ALU.mult),
                         [Brden, Bmgs], [Brden])
                    S.op("dve", lambda h: h.tensor_tensor(out=ob[:], in0=po[:], in1=rden[:], op=ALU.mult),
                         [Bpo, Brden], [Bob])
                    S.dma("sp", lambda h, hd=hd, q0=q0: h.dma_start(out=s_oT[16 + hd, :, q0:q0 + 512], in_=ob[:]),
                          [Bob], [BoT[16 + hd]])

        S.barrier()
        if stop == 3:
            S.wait_all("sp", S.dmabufs)
            return nc
        By = Buf("y")
        with ExitStack() as s3:
            psum_alloc(s3, 8, 0, 3)
            oT = sbuf(s3, "oT", [128, KC, 512], BF16)
            wo = [sbuf(s3, "wo%d" % i, [128, KC, 512], BF16) for i in range(2)]
            mix = sbuf(s3, "mix", [128, 4, D], F32)
            ggb = sbuf(s3, "ggb", [128, D], F32)
            xo = sbuf(s3, "xo", [128, D], F32)
            junk = sbuf(s3, "junk", [128, D], BF16)
            ss3 = sbuf(s3, "ss3", [128, 1], F32)
            BoTsb, Bmix, Bggb, Bxo, Bjunk, Bss3 = (Buf(n) for n in ("oTsb", "mix", "ggb", "xo", "junk", "ss3"))
            Bwo = [Buf("wo0"), Buf("wo1")]
            S.dma("sp", lambda h: h.dma_start(out=ggb[:], in_=s_gg[0:1, :].partition_broadcast(128)), [Bsgg], [Bggb])
            for so in range(2):
                o0 = so * 512
                S.dma("sp", lambda h, o0=o0: h.dma_start(
                    out=oT[:], in_=s_oT[:, :, o0:o0 + 512].rearrange("c p t -> p c t")), BoT, [BoTsb])
                for ctile in range(8):
                    i = ctile % 2
                    S.dma("sp", lambda h, i=i, ctile=ctile: h.dma_start(out=wo[i][:], in_=s_wout[ctile]),
                          [Bsw["wout"]], [Bwo[i]])
                    for tt in range(4):
                        ps, Bps = bank(0, 7)

                        def mm(h, ps=ps, tt=tt, i=i):
                            ins = None
                            for fc in range(KC):
                                ins = h.matmul(ps[:], lhsT=oT[:, fc, tt * 128:(tt + 1) * 128], rhs=wo[i][:, fc, :],
                                               start=(fc == 0), stop=(fc == KC - 1))
                            return ins
                        S.op("pe", mm, [BoTsb, Bwo[i]], [Bps])
                        S.op("act", lambda h, ps=ps, tt=tt, ctile=ctile: h.activation(
                            out=mix[:, tt, ctile * 512:(ctile + 1) * 512], in_=ps[:], func=AF.Copy), [Bps], [Bmix])
                for tt in range(4):
                    r0 = o0 + tt * 128
                    S.op("act", lambda h, tt=tt: h.activation(out=junk[:], in_=mix[:, tt, :], func=AF.Square,
                                                              accum_out=ss3[:, 0:1]), [Bmix], [Bjunk, Bss3])
                    S.op("act", lambda h: h.activation(out=ss3[:, 0:1], in_=ss3[:, 0:1], func=AF.Sqrt, scale=1.0 / D,
                                                       bias=epsb[:, 0:1]), [Bss3, Bconst], [Bss3])
                    S.op("dve", lambda h: h.reciprocal(out=ss3[:, 0:1], in_=ss3[:, 0:1]), [Bss3], [Bss3])
                    S.dma("sp", lambda h, r0=r0: h.dma_start(out=xo[:], in_=xl[OWN0 + r0:OWN0 + r0 + 128, :]), [], [Bxo])
                    S.op("dve", lambda h, tt=tt: h.scalar_tensor_tensor(
                        out=mix[:, tt, :], in0=mix[:, tt, :], scalar=ss3[:, 0:1], in1=ggb[:], op0=ALU.mult, op1=ALU.mult),
                        [Bmix, Bss3, Bggb], [Bmix])
                    S.op("dve", lambda h, tt=tt: h.tensor_tensor(out=xo[:], in0=xo[:], in1=mix[:, tt, :], op=ALU.add),
                         [Bxo, Bmix], [Bxo])
                    S.dma("sp", lambda h, r0=r0: h.dma_start(out=y[r0:r0 + 128, :], in_=xo[:]), [Bxo], [By])
            S.wait_all("sp", [By])
            if dev:
                S.wait_all("sp", Blat + [BSsn, Bcqn, Bmg] + BoT)
    return nc


def _tile_cols(W, tw):
    K, N = W.shape
    n = N // tw
    return np.ascontiguousarray(W.reshape(K // 128, 128, n, tw).transpose(2, 1, 0, 3))


def prep_shared(inp):
    f = lambda a: np.asarray(a, dtype=np.float32)
    w_in = f(inp["w_in"])
    o = np.cumsum([0, 1024, 1024, 2048, 16, 2048, 1536, 512, 64, 2048])
    gq_, gk_, gv_, alr_, gg_, cq_, ckv_, kr_, mg_ = [w_in[:, o[i]:o[i + 1]] for i in range(9)]
    swap = np.concatenate([np.arange(32, 64), np.arange(0, 32)])
    sh = {}
    sh["wkfm"] = _tile_cols(np.concatenate([ckv_, kr_, kr_[:, swap]], 1), 128)
    sh["walr"] = np.ascontiguousarray(alr_.reshape(KC, 128, 16).transpose(1, 0, 2))
    sh["wktm"] = _tile_cols(np.concatenate([gk_, gv_], 1), 256)
    sh["wqfm"] = _tile_cols(np.concatenate([gq_, gg_, cq_, mg_], 1), 128)
    wuq = f(inp["w_uq"]).reshape(1536, NH, 192)
    wuq = np.concatenate([wuq[:, :, 0:128], wuq[:, :, 128:192], wuq[:, :, 128:192][:, :, swap]], 2)
    sh["wuq"] = np.ascontiguousarray(wuq.reshape(12, 128, NH, 256).transpose(2, 1, 0, 3))
    wukv = f(inp["w_ukv"]).reshape(512, NH, 256)
    sh["wukv"] = np.ascontiguousarray(wukv.reshape(4, 128, NH, 256).transpose(2, 1, 0, 3))
    sh["wout"] = _tile_cols(f(inp["w_out"]), 512)
    sh["wada"] = _tile_cols(f(inp["w_ada"]), 256)
    sh["cvec"] = np.ascontiguousarray(f(inp["c"]).reshape(KC, 128).T)
    sh["bada"] = f(inp["b_ada"]).reshape(1, -1)
    sh["gpre"] = f(inp["g_pre"]).reshape(1, -1)
    sh["gpost"] = f(inp["g_post"]).reshape(1, -1)
    sh["wup"] = np.concatenate([f(inp["w_alpha_up"]), f(inp["b_alpha"]).reshape(1, -1)], 0)
    sh["gq"] = np.ascontiguousarray(f(inp["g_q_norm"]).reshape(12, 128).T)
    sh["gkv"] = np.ascontiguousarray(f(inp["g_kv_norm"]).reshape(4, 128).T)
    sh["ggla"] = f(inp["g_gla_out"]).reshape(128, 1)
    invf = (10000.0 ** (-np.arange(32, dtype=np.float32) / 32)).astype(np.float32)
    sh["invf"] = np.concatenate([invf, invf]).reshape(64, 1)
    sh["sgn"] = np.concatenate([-np.ones(32, np.float32), np.ones(32, np.float32)]).reshape(64, 1)
    sh["ident"] = np.eye(128, dtype=np.float32)
    s_ = np.arange(128)[:, None]
    t_ = np.arange(128)[None, :]
    sh["tri"] = (((s_ > t_) & (s_ // 64 == t_ // 64)).astype(np.float32) * np.float32(-1.0 / 16)).astype(np.float32)
    sh["cind"] = ((s_ // 64 == np.arange(2)[None, :]).astype(np.float32) * np.float32(-1.0 / 16)).astype(np.float32)
    return sh


def prep_core(inp, r, nblk):
    L = nblk * BLK
    x = np.asarray(inp["x"], dtype=np.float32)[0]
    pos = np.asarray(inp["positions"], dtype=np.int32)[0]
    n_real = (r + 1) * BLK
    off = L - n_real
    xl = np.zeros((L, D), np.float32)
    xl[off:] = x[:n_real]
    pl = np.zeros((L,), np.int32)
    pl[off:] = pos[:n_real]
    tokv = (np.arange(L) >= off)
    valid = np.ascontiguousarray(tokv.reshape(L // 128, 128).T.astype(np.float32))
    return {"xl": xl, "pos64": np.ascontiguousarray(np.broadcast_to(pl, (64, L))),
            "valid": valid, "maskb": np.where(valid > 0, np.float32(0.0), np.float32(NEG)).astype(np.float32)}


def run(inp, ncores, dev=False, stop=99):
    nblk = ncores
    nc = build(nblk, dev=dev, stop=stop)
    sh = prep_shared(inp)
    in_maps = []
    for r in range(ncores):
        m = dict(sh)
        m.update(prep_core(inp, r, nblk))
        in_maps.append(m)
    res = run_bass_kernel_spmd(nc, in_maps, core_ids=list(range(ncores)))
    out = np.concatenate([np.asarray(res.results[r]["y"], dtype=np.float32) for r in range(ncores)], 0)
    if dev:
        return out[None], res.results
    return out[None]


def kernel(**inputs):
    return run(inputs, 8)
```

```python
import numpy as np
from contextlib import ExitStack
import concourse.bass as bass
import concourse.mybir as mybir
from concourse.bass_utils import run_bass_kernel_spmd

F32 = mybir.dt.float32
BF16 = mybir.dt.bfloat16
I32 = mybir.dt.int32
AF = mybir.ActivationFunctionType
ALU = mybir.AluOpType

D = 4096
KC = 32
BLK = 1024
EPS = 1e-6
NH = 16
TWO_PI = float(2 * np.pi)
C1 = 6.28125
C2 = float(2 * np.pi - 6.28125)
NEG = -30000.0


class SemGroup:
    __slots__ = ("name", "sem", "cnt")

    def __init__(self, name):
        self.name = name
        self.sem = None
        self.cnt = 0


class Buf:
    __slots__ = ("name", "writer", "readers", "grp")

    def __init__(self, name, grp=None):
        self.name = name
        self.writer = None
        self.readers = []
        self.grp = grp if grp is not None else SemGroup(name)


class Sched:
    def __init__(self, nc, stack):
        self.nc = nc
        self.stack = stack
        self.engs = {}
        for name, h in (("pe", nc.tensor), ("act", nc.scalar), ("dve", nc.vector),
                        ("pool", nc.gpsimd), ("sp", nc.sync)):
            sem = stack.enter_context(nc.semaphore("prog_" + name))
            self.engs[name] = dict(h=h, sem=sem, cnt=0, waited={})
        self.nsem = 0
        self.dmabufs = []

    def _wait(self, e, tok):
        sem, val, grp = tok
        if grp is not None:
            val = grp.cnt
        E = self.engs[e]
        key = id(sem)
        if E["waited"].get(key, 0) >= val:
            return
        E["h"].wait_ge(sem, val)
        E["waited"][key] = val

    def _deps(self, e, reads, writes):
        for b in reads:
            if b.writer is not None:
                self._wait(e, b.writer)
        for b in writes:
            if b.writer is not None:
                self._wait(e, b.writer)
            for t in b.readers:
                self._wait(e, t)

    def _commit(self, tok, reads, writes):
        for b in reads:
            b.readers.append(tok)
            if len(b.readers) > 12:
                best = {}
                for s, v, g in b.readers:
                    k = id(s)
                    if k not in best or best[k][1] < v:
                        best[k] = (s, v, g)
                b.readers = list(best.values())
        for b in writes:
            b.writer = tok
            b.readers = []

    def op(self, e, fn, reads=(), writes=()):
        self._deps(e, reads, writes)
        E = self.engs[e]
        ins = fn(E["h"])
        E["cnt"] += 1
        ins.then_inc(E["sem"], 1)
        tok = (E["sem"], E["cnt"], None)
        self._commit(tok, reads, writes)
        return tok

    def dma(self, q, fn, reads, writes):
        self._deps(q, reads, writes)
        own = writes[0]
        g = own.grp
        if own not in self.dmabufs:
            self.dmabufs.append(own)
        if g.sem is None:
            g.sem = self.stack.enter_context(self.nc.semaphore("d%d" % self.nsem))
            self.nsem += 1
        inss = fn(self.engs[q]["h"])
        if not isinstance(inss, (list, tuple)):
            inss = [inss]
        for ins in inss:
            ins.then_inc(g.sem, 16)
            g.cnt += 16
        tok = (g.sem, g.cnt, g)
        self._commit(tok, reads, writes)
        return tok

    def barrier(self):
        toks = [(E["sem"], E["cnt"], None) for E in self.engs.values() if E["cnt"] > 0]
        grps = {}
        for b in self.dmabufs:
            if b.grp.sem is not None:
                grps[id(b.grp)] = b.grp
        for e in self.engs:
            for t in toks:
                self._wait(e, t)
            for g in grps.values():
                self._wait(e, (g.sem, g.cnt, g))

    def wait_all(self, e, bufs):
        for b in bufs:
            if b.writer is not None:
                self._wait(e, b.writer)


def build(nblk, dev=False, stop=99):
    L = nblk * BLK
    NT = L // 128
    NS = L // 512
    NCH = L // 64
    OWN0 = L - BLK
    nc = bass.Bass("TRN2", target_bir_lowering=False)

    def din(name, shape, dt=F32):
        return nc.dram_tensor(name, list(shape), dt, kind="ExternalInput").ap()

    def dscr(name, shape, dt=BF16):
        return nc.dram_tensor(name, list(shape), dt,
                              kind=("ExternalOutput" if dev else "Internal")).ap()

    xl = din("xl", [L, D])
    pos64 = din("pos64", [64, L], I32)
    valid_d = din("valid", [128, NT])
    maskb_d = din("maskb", [128, NT])
    cvec = din("cvec", [128, KC])
    bada = din("bada", [1, 3 * D])
    gpre = din("gpre", [1, D])
    gpost = din("gpost", [1, D])
    wada = din("wada", [24, 128, KC, 512])
    wkfm = din("wkfm", [5, 128, KC, 128])
    walr_d = din("walr", [128, KC, 16])
    wktm = din("wktm", [12, 128, KC, 256])
    wqfm = din("wqfm", [52, 128, KC, 128])
    wuq = din("wuq", [NH, 128, 12, 256])
    wukv = din("wukv", [NH, 128, 4, 256])
    wout = din("wout", [8, 128, KC, 512])
    wup_d = din("wup", [17, 1024])
    gq_d = din("gq", [128, 12])
    gkv_d = din("gkv", [128, 4])
    ggla_d = din("ggla", [128, 1])
    invf_d = din("invf", [64, 1])
    sgn_d = din("sgn", [64, 1])
    ident_d = din("ident", [128, 128])
    tri_d = din("tri", [128, 128])
    cind_d = din("cind", [128, 2])
    y = nc.dram_tensor("y", [BLK, D], F32, kind="ExternalOutput").ap()

    s_wkfm = nc.dram_tensor("s_wkfm", [5, 128, KC, 128], BF16).ap()
    s_wktm = nc.dram_tensor("s_wktm", [12, 128, KC, 256], BF16).ap()
    s_wqfm = nc.dram_tensor("s_wqfm", [52, 128, KC, 128], BF16).ap()
    s_wuq = nc.dram_tensor("s_wuq", [NH, 128, 12, 256], BF16).ap()
    s_wukv = nc.dram_tensor("s_wukv", [NH, 128, 4, 256], BF16).ap()
    s_wout = nc.dram_tensor("s_wout", [8, 128, KC, 512], BF16).ap()
    s_lat = dscr("s_lat", [5, 128, L])
    s_Ssn = dscr("s_Ssn", [16, 128, 8 * 128])
    s_cqn = dscr("s_cqn", [12, 128, BLK])
    s_mg = dscr("s_mg", [16, 128, BLK])
    s_oT = dscr("s_oT", [KC, 128, BLK])
    s_gg = nc.dram_tensor("s_gg", [1, D], F32).ap()

    with ExitStack() as st:
        S = Sched(nc, st)

        def sbuf(stack, name, shape, dt):
            return stack.enter_context(nc.sbuf_tensor("t_" + name, list(shape), dt))

        def cast2d(ap, n):
            names = " ".join("a%d" % i for i in range(n))
            flat = ap.rearrange("%s -> (%s)" % (names, names))
            return flat.rearrange("(r c) -> r c", c=2048)

        P = {}
        rot = {"i": 0}

        def psum_alloc(stack, n32, n16, nrot):
            P["pb"] = [stack.enter_context(nc.psum_tensor("pb%d_%d" % (rot["i"], i), [128, 512], F32)) for i in range(n32)]
            P["Bpb"] = [Buf("pb%d" % i) for i in range(n32)]
            P["ptr"] = [stack.enter_context(nc.psum_tensor("pt%d_%d" % (rot["i"], i), [128, 1024], BF16)) for i in range(n16)]
            P["Bptr"] = [Buf("ptr%d" % i) for i in range(n16)]
            P["nrot"] = nrot
            rot["i"] += 1000

        def bank(lo=0, hi=None):
            hi = P["nrot"] if hi is None else hi
            i = lo + rot["i"] % (hi - lo)
            rot["i"] += 1
            return P["pb"][i], P["Bpb"][i]

        identb = sbuf(st, "identb", [128, 128], BF16)
        onesb = sbuf(st, "onesb", [128, 128], BF16)
        onesf = sbuf(st, "onesf", [1, 128], F32)
        one11 = sbuf(st, "one11", [1, 1], F32)
        epsb = sbuf(st, "epsb", [128, 1], F32)
        Gp = sbuf(st, "Gp", [128, KC], F32)
        Sp = sbuf(st, "Sp", [128, KC], F32)
        validt = sbuf(st, "validt", [128, NT], F32)
        maskbt = sbuf(st, "maskbt", [128, NT], F32)
        gq = sbuf(st, "gq", [128, 12], F32)
        gkv = sbuf(st, "gkv", [128, 4], F32)
        ggla = sbuf(st, "ggla", [128, 1], F32)
        invf = sbuf(st, "invf", [64, 1], F32)
        sgn = sbuf(st, "sgn", [64, 1], F32)
        Bconst = Buf("const")
        BGS = Buf("GS")

        with ExitStack() as s0:
            tmpf = sbuf(s0, "tmpf", [128, 128], F32)
            Btmpf = Buf("tmpf")
            S.dma("sp", lambda h: h.dma_start(out=tmpf[:], in_=ident_d[:, :]), [], [Btmpf])
            S.op("dve", lambda h: h.tensor_copy(out=identb[:], in_=tmpf[:]), [Btmpf], [Bconst])
        S.barrier()
        S.op("dve", lambda h: h.memset(onesb[:], 1.0), [], [Bconst])
        S.op("dve", lambda h: h.memset(onesf[:], 1.0), [], [Bconst])
        S.op("dve", lambda h: h.memset(one11[:], 1.0), [], [Bconst])
        S.op("dve", lambda h: h.memset(epsb[:], EPS), [], [Bconst])
        for dst, src in ((validt, valid_d), (maskbt, maskb_d), (gq, gq_d), (gkv, gkv_d),
                         (ggla, ggla_d), (invf, invf_d), (sgn, sgn_d)):
            S.dma("sp", (lambda d_, s_: (lambda h: h.dma_start(out=d_[:], in_=s_[:, :])))(dst, src),
                  [], [Bconst])

        Bsw = {}
        for name, src, dst, n in (("wkfm", wkfm, s_wkfm, 4), ("wktm", wktm, s_wktm, 4),
                                  ("wqfm", wqfm, s_wqfm, 4), ("wuq", wuq, s_wuq, 4),
                                  ("wukv", wukv, s_wukv, 4), ("wout", wout, s_wout, 4)):
            Bsw[name] = Buf("s_" + name)
            sv, dv = cast2d(src, n), cast2d(dst, n)
            R = sv.shape[0]
            step = 4096

            def castfn(h, sv=sv, dv=dv, R=R, step=step):
                out = []
                for r0 in range(0, R, step):
                    r1 = min(R, r0 + step)
                    out.append(h.dma_start(out=dv[r0:r1, :], in_=sv[r0:r1, :]))
                return out
            S.dma("pool", castfn, [], [Bsw[name]])

        with ExitStack() as s0:
            psum_alloc(s0, 8, 0, 3)
            ct = sbuf(s0, "ct", [128, KC], F32)
            scb = sbuf(s0, "scb", [128, KC], BF16)
            modrow = sbuf(s0, "modrow", [1, 3 * D], F32)
            badarow = sbuf(s0, "badarow", [1, 3 * D], F32)
            grow = sbuf(s0, "grow", [1, D], F32)
            wa = [sbuf(s0, "wa%d" % i, [128, KC, 512], BF16) for i in range(2)]
            gsb = sbuf(s0, "gsb", [128, 2 * KC], F32)
            Bct, Bscb, Bmod, Bbada, Bgrow = Buf("ct"), Buf("scb"), Buf("mod"), Buf("bada"), Buf("grow")
            Bwa = [Buf("wa0"), Buf("wa1")]
            S.dma("sp", lambda h: h.dma_start(out=ct[:], in_=cvec[:, :]), [], [Bct])
            S.dma("sp", lambda h: h.dma_start(out=badarow[:], in_=bada[:, :]), [], [Bbada])
            S.dma("sp", lambda h: h.dma_start(out=grow[:], in_=gpre[:, :]), [], [Bgrow])
            S.op("act", lambda h: h.activation(out=scb[:], in_=ct[:], func=AF.Silu), [Bct], [Bscb])
            for nt in range(24):
                w, Bw = wa[nt % 2], Bwa[nt % 2]
                S.dma("pool", lambda h, w=w, nt=nt: h.dma_start(
                    out=w[:].rearrange("p k c -> p (k c)").rearrange("p (a b) -> p a b", b=2048),
                    in_=wada[nt].rearrange("p k c -> p (k c)").rearrange("p (a b) -> p a b", b=2048)),
                    [], [Bw])
                ps, Bps = bank()

                def gemv(h, ps=ps, w=w):
                    ins = None
                    for kc in range(KC):
                        ins = h.matmul(ps[0:1, :], lhsT=scb[:, kc:kc + 1], rhs=w[:, kc, :],
                                       start=(kc == 0), stop=(kc == KC - 1))
                    return ins
                S.op("pe", gemv, [Bscb, Bw], [Bps])
                S.op("dve", lambda h, ps=ps, nt=nt: h.tensor_tensor(
                    out=modrow[0:1, nt * 512:(nt + 1) * 512], in0=ps[0:1, :],
                    in1=badarow[0:1, nt * 512:(nt + 1) * 512], op=ALU.add), [Bps, Bbada], [Bmod])
            S.op("dve", lambda h: h.scalar_tensor_tensor(
                out=modrow[0:1, D:2 * D], in0=modrow[0:1, D:2 * D], scalar=1.0, in1=grow[0:1, :],
                op0=ALU.add, op1=ALU.mult), [Bmod, Bgrow], [Bmod])
            S.dma("sp", lambda h: h.dma_start(out=grow[:], in_=gpost[:, :]), [], [Bgrow])
            S.op("dve", lambda h: h.tensor_tensor(
                out=modrow[0:1, 2 * D:3 * D], in0=modrow[0:1, 2 * D:3 * D], in1=grow[0:1, :],
                op=ALU.mult), [Bmod, Bgrow], [Bmod])
            Bsgg = Buf("s_gg")
            S.dma("sp", lambda h: h.dma_start(out=s_gg[:, :], in_=modrow[0:1, 2 * D:3 * D]), [Bmod], [Bsgg])
            ps, Bps = bank()

            def rows2p(h, ps=ps):
                ins = None
                for j, off in enumerate((D, 0)):
                    for kc in range(KC):
                        ins = h.matmul(ps[:, j * KC + kc:j * KC + kc + 1],
                                       lhsT=modrow[0:1, off + kc * 128:off + (kc + 1) * 128],
                                       rhs=one11[0:1, 0:1], start=True, stop=True)
                return ins
            S.op("pe", rows2p, [Bmod, Bconst], [Bps])
            S.op("dve", lambda h, ps=ps: h.tensor_copy(out=Gp[:], in_=ps[:, 0:KC]), [Bps], [BGS])
            S.op("dve", lambda h, ps=ps: h.tensor_copy(out=Sp[:], in_=ps[:, KC:2 * KC]), [Bps], [BGS])

        S.barrier()
        if stop == 0:
            S.wait_all("sp", S.dmabufs)
            return nc
        def make_hT(xrow0, xs, Bxs, xn, Bxn, ssq, Bssq, hT, BhT, BhT2):
            for tt in range(4):
                x_, Bx_ = xs[tt % 2], Bxs[tt % 2]
                n_, Bn_ = xn[tt % 2], Bxn[tt % 2]
                r0 = xrow0 + tt * 128
                S.dma("sp", lambda h, x_=x_, r0=r0: h.dma_start(out=x_[:], in_=xl[r0:r0 + 128, :]), [], [Bx_])
                S.op("act", lambda h, x_=x_, n_=n_: h.activation(
                    out=n_[:], in_=x_[:], func=AF.Square, accum_out=ssq[:, 0:1]), [Bx_], [Bn_, Bssq])
                S.op("act", lambda h: h.activation(out=ssq[:, 0:1], in_=ssq[:, 0:1], func=AF.Sqrt,
                                                   scale=1.0 / D, bias=epsb[:, 0:1]), [Bssq, Bconst], [Bssq])
                S.op("dve", lambda h: h.reciprocal(out=ssq[:, 0:1], in_=ssq[:, 0:1]), [Bssq], [Bssq])
                S.op("act", lambda h, x_=x_, n_=n_: h.activation(
                    out=n_[:], in_=x_[:], func=AF.Copy, scale=ssq[:, 0:1]), [Bx_, Bssq], [Bn_])
                for g in range(8):
                    half = g % 2
                    pt = P["ptr"][half][:, 0:512]

                    def tr(h, pt=pt, n_=n_, g=g):
                        ins = None
                        for j in range(4):
                            kc = g * 4 + j
                            ins = h.transpose(out=pt[:, j * 128:(j + 1) * 128],
                                              in_=n_[:, kc * 128:(kc + 1) * 128], identity=identb[:])
                        return ins
                    S.op("pe", tr, [Bn_, Bconst], [P["Bptr"][half]])
                    for j in range(4):
                        kc = g * 4 + j
                        if half == 0:
                            S.op("act", lambda h, pt=pt, j=j, kc=kc, tt=tt: h.activation(
                                out=hT[:, kc, tt * 128:(tt + 1) * 128], in_=pt[:, j * 128:(j + 1) * 128],
                                func=AF.Identity, scale=Gp[:, kc:kc + 1], bias=Sp[:, kc:kc + 1]),
                                [P["Bptr"][half], BGS], [BhT])
                        else:
                            S.op("dve", lambda h, pt=pt, j=j, kc=kc, tt=tt: h.tensor_scalar(
                                out=hT[:, kc, tt * 128:(tt + 1) * 128], in0=pt[:, j * 128:(j + 1) * 128],
                                scalar1=Gp[:, kc:kc + 1], scalar2=Sp[:, kc:kc + 1], op0=ALU.mult, op1=ALU.add),
                                [P["Bptr"][half], BGS], [BhT2])

        def rope_tables(s1, tok0, n, pre):
            posi = sbuf(s1, pre + "posi", [64, n], I32)
            ang = sbuf(s1, pre + "ang", [64, n], F32)
            kf = sbuf(s1, pre + "kf", [64, n], F32)
            cosT = sbuf(s1, pre + "cosT", [64, n], F32)
            sinT = sbuf(s1, pre + "sinT", [64, n], F32)
            Bp, Ba, Bk, Bc, Bs = (Buf(pre + x) for x in ("posi", "ang", "kf", "cos", "sin"))

            def emit(tok0):
                S.dma("sp", lambda h: h.dma_start(out=posi[:], in_=pos64[:, tok0:tok0 + n]), [], [Bp])
                for shift, dst, Bd, use_sgn in ((0.0, sinT, Bs, True), (float(np.pi / 2), cosT, Bc, False)):
                    S.op("dve", lambda h: h.tensor_copy(out=ang[:], in_=posi[:]), [Bp], [Ba])
                    S.op("dve", lambda h, shift=shift: h.tensor_scalar(
                        out=ang[:], in0=ang[:], scalar1=invf[:, 0:1], scalar2=shift,
                        op0=ALU.mult, op1=ALU.add), [Ba, Bconst], [Ba])
                    S.op("dve", lambda h: h.tensor_scalar(out=kf[:], in0=ang[:], scalar1=float(1.0 / TWO_PI),
                                                          scalar2=None, op0=ALU.mult), [Ba], [Bk])
                    S.op("dve", lambda h: h.tensor_copy(out=posi[:], in_=kf[:]), [Bk], [Bp])
                    S.op("dve", lambda h: h.tensor_copy(out=kf[:], in_=posi[:]), [Bp], [Bk])
                    S.op("dve", lambda h: h.scalar_tensor_tensor(out=ang[:], in0=kf[:], scalar=-C1, in1=ang[:],
                                                                 op0=ALU.mult, op1=ALU.add), [Bk, Ba], [Ba])
                    S.op("dve", lambda h: h.scalar_tensor_tensor(out=ang[:], in0=kf[:], scalar=-C2, in1=ang[:],
                                                                 op0=ALU.mult, op1=ALU.add), [Bk, Ba], [Ba])
                    S.op("dve", lambda h: h.tensor_scalar(out=kf[:], in0=ang[:], scalar1=float(np.pi),
                                                          scalar2=-TWO_PI, op0=ALU.is_gt, op1=ALU.mult), [Ba], [Bk])
                    S.op("dve", lambda h: h.tensor_tensor(out=ang[:], in0=ang[:], in1=kf[:], op=ALU.add), [Ba, Bk], [Ba])
                    S.op("dve", lambda h: h.tensor_scalar(out=kf[:], in0=ang[:], scalar1=float(-np.pi),
                                                          scalar2=TWO_PI, op0=ALU.is_lt, op1=ALU.mult), [Ba], [Bk])
                    S.op("dve", lambda h: h.tensor_tensor(out=ang[:], in0=ang[:], in1=kf[:], op=ALU.add), [Ba, Bk], [Ba])
                    S.op("dve", lambda h: h.tensor_scalar(out=ang[:], in0=ang[:], scalar1=3.1415925, scalar2=-3.1415925,
                                                          op0=ALU.min, op1=ALU.max), [Ba], [Ba])
                    if use_sgn:
                        S.op("act", lambda h, dst=dst: h.activation(out=dst[:], in_=ang[:], func=AF.Sin,
                                                                    scale=sgn[:, 0:1]), [Ba, Bconst], [Bd])
                    else:
                        S.op("act", lambda h, dst=dst: h.activation(out=dst[:], in_=ang[:], func=AF.Sin), [Ba], [Bd])
                    if use_sgn:
                        S.dma("sp", lambda h: h.dma_start(out=posi[:], in_=pos64[:, tok0:tok0 + n]), [], [Bp])
            return emit, cosT, sinT, Bc, Bs

        Blat = [Buf("lat%d" % s) for s in range(NS)]
        latgrp = SemGroup("latgrp")
        for b in Blat:
            b.grp = latgrp
        BSsn = Buf("Ssn")
        with ExitStack() as s1:
            psum_alloc(s1, 6, 2, 2)
            pb, Bpb = P["pb"], P["Bpb"]
            xs = [sbuf(s1, "xs%d" % i, [128, D], F32) for i in range(2)]
            xn = [sbuf(s1, "xn%d" % i, [128, D], BF16) for i in range(2)]
            Bxs, Bxn = [Buf("xs0"), Buf("xs1")], [Buf("xn0"), Buf("xn1")]
            ssq = sbuf(s1, "ssq", [128, 1], F32)
            Bssq = Buf("ssq")
            hT = sbuf(s1, "hT", [128, KC, 512], BF16)
            BhT = Buf("hT")
            BhT2 = Buf("hT2")
            wfm = [sbuf(s1, "wfm%d" % i, [128, KC, 128], BF16) for i in range(2)]
            Bwfm = [Buf("wfm0"), Buf("wfm1")]
            wtm = [sbuf(s1, "wtm%d" % i, [128, KC, 256], BF16) for i in range(2)]
            Bwtm = [Buf("wtm0"), Buf("wtm1")]
            walr = sbuf(s1, "walr", [128, KC, 16], BF16)
            Bwalr = Buf("walr")
            ckvf = sbuf(s1, "ckvf", [128, 4, 512], F32)
            sqb = sbuf(s1, "sqb", [128, 512], BF16)
            rstdb = sbuf(s1, "rstdb", [128, 512], F32)
            latbf = sbuf(s1, "latbf", [128, 4, 512], BF16)
            krf = sbuf(s1, "krf", [64, 512], F32)
            krs = sbuf(s1, "krs", [64, 512], F32)
            krb = sbuf(s1, "krb", [64, 512], BF16)
            Bckvf, Bsqb, Brstdb, Blatbf, Bkrf, Bkrs, Bkrb = (Buf(n) for n in
                                                             ("ckvf", "sqb", "rstdb", "latbf", "krf", "krs", "krb"))
            ktm = sbuf(s1, "ktm", [128, 4, 1024], BF16)
            vtm = sbuf(s1, "vtm", [128, 4, 2048], BF16)
            Bktm, Bvtm = Buf("ktm"), Buf("vtm")
            alrT = sbuf(s1, "alrT", [32, 512], F32)
            wup = sbuf(s1, "wup", [32, 1024], F32)
            lt = sbuf(s1, "lt", [128, 1024], F32)
            kdec = sbuf(s1, "kdec", [128, 1024], BF16)
            Sst = sbuf(s1, "Sst", [128, 8, 128], F32)
            Sbf = sbuf(s1, "Sbf", [128, 8, 128], BF16)
            dec = sbuf(s1, "dec", [128, 16], F32)
            tri = sbuf(s1, "tri", [128, 128], F32)
            cind = sbuf(s1, "cind", [128, 2], F32)
            BalrT, Bwup, Blt, Bkdec, BSst, BSbf, Bdec, Bgc = (Buf(n) for n in
                                                              ("alrT", "wup", "lt", "kdec", "Sst", "Sbf", "dec", "gc"))
            rope_emit, cosT, sinT, Bcos, Bsin = rope_tables(s1, 0, 512, "r1")

            S.dma("pool", lambda h: h.dma_start(out=walr[:], in_=walr_d[:, :, :]), [], [Bwalr])
            S.op("dve", lambda h: h.memset(alrT[:], 1.0), [], [BalrT])
            S.dma("sp", lambda h: h.dma_start(out=wup[0:17, :], in_=wup_d[:, :]), [], [Bwup])
            S.dma("sp", lambda h: h.dma_start(out=tri[:], in_=tri_d[:, :]), [], [Bgc])
            S.dma("sp", lambda h: h.dma_start(out=cind[:], in_=cind_d[:, :]), [], [Bgc])
            S.op("dve", lambda h: h.memset(Sst[:], 0.0), [], [BSst])

            wi = {"fm": 0, "tm": 0}

            def load_fm(src_ap, Bsrc):
                i = wi["fm"] % 2
                wi["fm"] += 1
                S.dma("sp", lambda h: h.dma_start(out=wfm[i][:], in_=src_ap), [Bsrc], [Bwfm[i]])
                return wfm[i], Bwfm[i]

            def load_tm(src_ap, Bsrc):
                i = wi["tm"] % 2
                wi["tm"] += 1
                S.dma("sp", lambda h: h.dma_start(out=wtm[i][:], in_=src_ap), [Bsrc], [Bwtm[i]])
                return wtm[i], Bwtm[i]

            for s in range(NS):
                t0 = s * 512
                make_hT(t0, xs, Bxs, xn, Bxn, ssq, Bssq, hT, BhT, BhT2)
                if stop == 10:
                    S.wait_all("sp", S.dmabufs)
                    return nc
                rope_emit(t0)
                if stop == 11:
                    S.wait_all("sp", S.dmabufs)
                    return nc
                psq, Bpsq = pb[2], Bpb[2]
                for ctile in range(4):
                    w, Bw = load_fm(s_wkfm[ctile], Bsw["wkfm"])
                    ps, Bps = bank()

                    def mm(h, ps=ps, w=w):
                        ins = None
                        for kc in range(KC):
                            ins = h.matmul(ps[:], lhsT=w[:, kc, :], rhs=hT[:, kc, :],
                                           start=(kc == 0), stop=(kc == KC - 1))
                        return ins
                    S.op("pe", mm, [Bw, BhT, BhT2], [Bps])
                    S.op("act", lambda h, ps=ps, ctile=ctile: h.activation(
                        out=ckvf[:, ctile, :], in_=ps[:], func=AF.Copy), [Bps], [Bckvf])
                    S.op("act", lambda h, ps=ps: h.activation(out=sqb[:], in_=ps[:], func=AF.Square), [Bps], [Bsqb])
                    S.op("pe", lambda h, ctile=ctile: h.matmul(psq[:], lhsT=onesb[:], rhs=sqb[:],
                                                               start=(ctile == 0), stop=(ctile == 3),
                                                               skip_group_check=True),
                         [Bsqb, Bconst], [Bpsq])
                S.op("act", lambda h: h.activation(out=rstdb[:], in_=psq[:], func=AF.Sqrt, scale=1.0 / 512,
                                                   bias=epsb[:, 0:1]), [Bpsq, Bconst], [Brstdb])
                S.op("dve", lambda h: h.reciprocal(out=rstdb[:], in_=rstdb[:]), [Brstdb], [Brstdb])
                for ctile in range(4):
                    S.op("dve", lambda h, ctile=ctile: h.scalar_tensor_tensor(
                        out=latbf[:, ctile, :], in0=ckvf[:, ctile, :], scalar=gkv[:, ctile:ctile + 1],
                        in1=rstdb[:], op0=ALU.mult, op1=ALU.mult), [Bckvf, Brstdb, Bconst], [Blatbf])
                S.dma("sp", lambda h, t0=t0: h.dma_start(
                    out=s_lat[0:4, :, t0:t0 + 512].rearrange("c p t -> p c t"), in_=latbf[:]),
                    [Blatbf], [Blat[s]])
                if stop == 12:
                    S.wait_all("sp", S.dmabufs)
                    return nc
                w, Bw = load_fm(s_wkfm[4], Bsw["wkfm"])
                for half, dst, Bd in ((0, krf, Bkrf), (1, krs, Bkrs)):
                    ps, Bps = bank()

                    def mm(h, ps=ps, w=w, half=half):
                        ins = None
                        for kc in range(KC):
                            ins = h.matmul(ps[0:64, :], lhsT=w[:, kc, half * 64:(half + 1) * 64], rhs=hT[:, kc, :],
                                           start=(kc == 0), stop=(kc == KC - 1))
                        return ins
                    S.op("pe", mm, [Bw, BhT, BhT2], [Bps])
                    tab, Btab = (cosT, Bcos) if half == 0 else (sinT, Bsin)
                    S.op("dve", lambda h, ps=ps, dst=dst, tab=tab: h.tensor_tensor(
                        out=dst[:], in0=ps[0:64, :], in1=tab[:], op=ALU.mult), [Bps, Btab], [Bd])
                S.op("dve", lambda h: h.tensor_tensor(out=krb[:], in0=krf[:], in1=krs[:], op=ALU.add),
                     [Bkrf, Bkrs], [Bkrb])
                S.dma("sp", lambda h, t0=t0: h.dma_start(out=s_lat[4, 0:64, t0:t0 + 512], in_=krb[:]),
                      [Bkrb], [Blat[s]])
                if stop == 13:
                    S.wait_all("sp", S.dmabufs)
                    return nc
                ps, Bps = bank()

                def mm(h, ps=ps):
                    ins = None
                    for kc in range(KC):
                        ins = h.matmul(ps[0:16, :], lhsT=walr[:, kc, :], rhs=hT[:, kc, :],
                                       start=(kc == 0), stop=(kc == KC - 1))
                    return ins
                S.op("pe", mm, [Bwalr, BhT, BhT2], [Bps])
                S.op("act", lambda h, ps=ps: h.activation(out=alrT[0:16, :], in_=ps[0:16, :], func=AF.Copy),
                     [Bps], [BalrT])
                if stop == 14:
                    S.wait_all("sp", S.dmabufs)
                    return nc
                for ctile in range(12):
                    w, Bw = load_tm(s_wktm[ctile], Bsw["wktm"])
                    for tt in range(4):
                        ps, Bps = bank()

                        def mm(h, ps=ps, w=w, tt=tt):
                            ins = None
                            for kc in range(KC):
                                ins = h.matmul(ps[:, 0:256], lhsT=hT[:, kc, tt * 128:(tt + 1) * 128], rhs=w[:, kc, :],
                                               start=(kc == 0), stop=(kc == KC - 1))
                            return ins
                        S.op("pe", mm, [Bw, BhT, BhT2], [Bps])
                        if ctile < 4:
                            S.op("act", lambda h, ps=ps, tt=tt, ctile=ctile: h.activation(
                                out=ktm[:, tt, ctile * 256:(ctile + 1) * 256], in_=ps[:, 0:256], func=AF.Copy),
                                [Bps], [Bktm])
                        else:
                            c2 = ctile - 4
                            tg = s * 4 + tt
                            S.op("act", lambda h, ps=ps, tt=tt, c2=c2, tg=tg: h.activation(
                                out=vtm[:, tt, c2 * 256:(c2 + 1) * 256], in_=ps[:, 0:256], func=AF.Copy,
                                scale=validt[:, tg:tg + 1]), [Bps, Bconst], [Bvtm])
                if stop == 15:
                    S.wait_all("sp", S.dmabufs)
                    return nc
                for tt in range(4):
                    for hf in range(2):
                        ps, Bps = pb[3], Bpb[3]
                        S.op("pe", lambda h, ps=ps, tt=tt, hf=hf: h.matmul(
                            ps[:], lhsT=alrT[0:17, tt * 128:(tt + 1) * 128], rhs=wup[0:17, hf * 512:(hf + 1) * 512],
                            start=True, stop=True), [BalrT, Bwup], [Bps])
                        S.op("act", lambda h, ps=ps, hf=hf: h.activation(
                            out=lt[:, hf * 512:(hf + 1) * 512], in_=ps[:], func=AF.Exp, scale=-1.0), [Bps], [Blt])
                    S.op("act", lambda h: h.activation(out=lt[:], in_=lt[:], func=AF.Ln, bias=1.0), [Blt], [Blt])
                    for hf in range(2):
                        ps, Bps = pb[3], Bpb[3]
                        S.op("pe", lambda h, ps=ps, hf=hf: h.matmul(
                            ps[:], lhsT=tri[:], rhs=lt[:, hf * 512:(hf + 1) * 512], start=True, stop=True),
                            [Bgc, Blt], [Bps])
                        S.op("act", lambda h, ps=ps, hf=hf: h.activation(
                            out=kdec[:, hf * 512:(hf + 1) * 512], in_=ps[:], func=AF.Exp), [Bps], [Bkdec])
                    S.op("dve", lambda h, tt=tt: h.tensor_tensor(out=kdec[:], in0=kdec[:], in1=ktm[:, tt, :], op=ALU.mult),
                         [Bkdec, Bktm], [Bkdec])
                    ps, Bps = pb[3], Bpb[3]

                    def mmlt(h, ps=ps):
                        ins = None
                        for pr in range(8):
                            ins = h.matmul(ps[:, pr * 2:pr * 2 + 2], lhsT=lt[:, pr * 128:(pr + 1) * 128], rhs=cind[:],
                                           start=True, stop=True)
                        return ins
                    S.op("pe", mmlt, [Blt, Bgc], [Bps])
                    S.op("act", lambda h, ps=ps: h.activation(out=dec[:], in_=ps[:, 0:16], func=AF.Exp), [Bps], [Bdec])
                    for c in range(2):
                        n = (s * 4 + tt) * 2 + c
                        r0 = c * 64
                        kvp = [pb[4], pb[5]]

                        def mmkv(h, r0=r0, tt=tt):
                            ins = None
                            for hd in range(NH):
                                pr, od = hd // 2, hd % 2
                                ins = h.matmul(kvp[pr // 4][od * 64:(od + 1) * 64, (pr % 4) * 128:(pr % 4 + 1) * 128],
                                               lhsT=kdec[r0:r0 + 64, hd * 64:(hd + 1) * 64],
                                               rhs=vtm[r0:r0 + 64, tt, hd * 128:(hd + 1) * 128],
                                               start=True, stop=True, tile_position=(r0, od * 64))
                            return ins
                        S.op("pe", mmkv, [Bkdec, Bvtm], [Bpb[4], Bpb[5]])
                        for pr in range(8):
                            S.op("dve", lambda h, pr=pr, c=c: h.scalar_tensor_tensor(
                                out=Sst[:, pr, :], in0=Sst[:, pr, :], scalar=dec[:, pr * 2 + c:pr * 2 + c + 1],
                                in1=kvp[pr // 4][:, (pr % 4) * 128:(pr % 4 + 1) * 128], op0=ALU.mult, op1=ALU.add),
                                [BSst, Bdec, Bpb[4 + pr // 4]], [BSst])
                        if n >= NCH - 16:
                            S.op("act", lambda h: h.activation(out=Sbf[:], in_=Sst[:], func=AF.Copy), [BSst], [BSbf])
                            S.dma("sp", lambda h, n=n: h.dma_start(
                                out=s_Ssn[n - (NCH - 16)], in_=Sbf[:].rearrange("p a b -> p (a b)")), [BSbf], [BSsn])

        S.barrier()
        if stop == 1:
            S.wait_all("sp", S.dmabufs)
            return nc
        Bcqn, Bmg = Buf("s_cqn"), Buf("s_mg")
        BoT = [Buf("s_oT%d" % i) for i in range(KC)]
        with ExitStack() as s1:
            psum_alloc(s1, 6, 2, 2)
            pb, Bpb = P["pb"], P["Bpb"]
            xs = [sbuf(s1, "bxs%d" % i, [128, D], F32) for i in range(2)]
            xn = [sbuf(s1, "bxn%d" % i, [128, D], BF16) for i in range(2)]
            Bxs, Bxn = [Buf("bxs0"), Buf("bxs1")], [Buf("bxn0"), Buf("bxn1")]
            ssq = sbuf(s1, "bssq", [128, 1], F32)
            Bssq = Buf("bssq")
            hT = sbuf(s1, "bhT", [128, KC, 512], BF16)
            BhT = Buf("bhT")
            BhT2 = Buf("bhT2")
            wfm = [sbuf(s1, "bwfm%d" % i, [128, KC, 128], BF16) for i in range(3)]
            Bwfm = [Buf("bwfm%d" % i) for i in range(3)]
            qT = sbuf(s1, "qT", [128, 8, 512], BF16)
            BqT = Buf("qT")
            Ssn = sbuf(s1, "Ssn", [128, 8, 8 * 128], BF16)
            BSs = Buf("Ssn_sb")
            osq = sbuf(s1, "osq", [128, 512], BF16)
            orst = sbuf(s1, "orst", [128, 512], F32)
            onT = sbuf(s1, "onT", [128, NH, 512], BF16)
            og = sbuf(s1, "og", [128, 512], BF16)
            cqb = sbuf(s1, "cqb", [128, 12, 512], BF16)
            cqf = sbuf(s1, "cqf", [128, 512], F32)
            mgb = sbuf(s1, "mgb", [128, 512], BF16)
            Bosq, Borst, BonT, Bog, Bcqb, Bcqf, Bmgb = (Buf(n) for n in
                                                        ("osq", "orst", "onT", "og", "cqb", "cqf", "mgb"))
            wj = {"i": 0}

            def load_q(idx):
                i = wj["i"] % 3
                wj["i"] += 1
                S.dma("sp", lambda h: h.dma_start(out=wfm[i][:], in_=s_wqfm[idx]), [Bsw["wqfm"]], [Bwfm[i]])
                return wfm[i], Bwfm[i]

            def fm_mm(w, Bw):
                ps, Bps = bank()

                def mm(h):
                    ins = None
                    for kc in range(KC):
                        ins = h.matmul(ps[:], lhsT=w[:, kc, :], rhs=hT[:, kc, :], start=(kc == 0), stop=(kc == KC - 1))
                    return ins
                S.op("pe", mm, [Bw, BhT, BhT2], [Bps])
                return ps, Bps

            for so in range(2):
                t0 = OWN0 + so * 512
                o0 = so * 512
                make_hT(t0, xs, Bxs, xn, Bxn, ssq, Bssq, hT, BhT, BhT2)
                for pr in range(8):
                    w, Bw = load_q(pr)
                    ps, Bps = fm_mm(w, Bw)
                    S.op("act", lambda h, ps=ps, pr=pr: h.activation(out=qT[:, pr, :], in_=ps[:], func=AF.Copy,
                                                                     scale=0.125), [Bps], [BqT])
                S.dma("sp", lambda h, so=so: h.dma_start(
                    out=Ssn[:], in_=s_Ssn[so * 8:(so + 1) * 8].rearrange("n p f -> p n f")), [BSsn], [BSs])
                for hd in range(NH):
                    pr, od = hd // 2, hd % 2
                    ps, Bps = bank()

                    def mmo(h, ps=ps, pr=pr, od=od):
                        ins = None
                        for cn in range(8):
                            ins = h.matmul(ps[:, cn * 64:(cn + 1) * 64],
                                           lhsT=Ssn[od * 64:(od + 1) * 64, cn, pr * 128:(pr + 1) * 128],
                                           rhs=qT[od * 64:(od + 1) * 64, pr, cn * 64:(cn + 1) * 64],
                                           start=True, stop=True, tile_position=(od * 64, 0))
                        return ins
                    S.op("pe", mmo, [BSs, BqT], [Bps])
                    S.op("act", lambda h, ps=ps: h.activation(out=osq[:], in_=ps[:], func=AF.Square), [Bps], [Bosq])
                    S.op("pe", lambda h: h.matmul(pb[2][:], lhsT=onesb[:], rhs=osq[:], start=True, stop=True),
                         [Bosq, Bconst], [Bpb[2]])
                    S.op("act", lambda h: h.activation(out=orst[:], in_=pb[2][:], func=AF.Sqrt, scale=1.0 / 128,
                                                       bias=epsb[:, 0:1]), [Bpb[2], Bconst], [Borst])
                    S.op("dve", lambda h: h.reciprocal(out=orst[:], in_=orst[:]), [Borst], [Borst])
                    S.op("dve", lambda h, ps=ps, hd=hd: h.scalar_tensor_tensor(
                        out=onT[:, hd, :], in0=ps[:], scalar=ggla[:, 0:1], in1=orst[:], op0=ALU.mult, op1=ALU.mult),
                        [Bps, Borst, Bconst], [BonT])
                for hd in range(NH):
                    w, Bw = load_q(8 + hd)
                    ps, Bps = fm_mm(w, Bw)
                    S.op("act", lambda h, ps=ps: h.activation(out=og[:], in_=ps[:], func=AF.Silu), [Bps], [Bog])
                    S.op("dve", lambda h, hd=hd: h.tensor_tensor(out=onT[:, hd, :], in0=onT[:, hd, :], in1=og[:],
                                                                  op=ALU.mult), [BonT, Bog], [BonT])
                    S.dma("sp", lambda h, hd=hd, o0=o0: h.dma_start(out=s_oT[hd, :, o0:o0 + 512], in_=onT[:, hd, :]),
                          [BonT], [BoT[hd]])
                for ctile in range(12):
                    w, Bw = load_q(24 + ctile)
                    ps, Bps = fm_mm(w, Bw)
                    S.op("act", lambda h, ps=ps, ctile=ctile: h.activation(out=cqb[:, ctile, :], in_=ps[:], func=AF.Copy),
                         [Bps], [Bcqb])
                    S.op("act", lambda h, ps=ps: h.activation(out=osq[:], in_=ps[:], func=AF.Square), [Bps], [Bosq])
                    S.op("pe", lambda h, ctile=ctile: h.matmul(pb[2][:], lhsT=onesb[:], rhs=osq[:],
                                                               start=(ctile == 0), stop=(ctile == 11),
                                                               skip_group_check=True),
                         [Bosq, Bconst], [Bpb[2]])
                S.op("act", lambda h: h.activation(out=orst[:], in_=pb[2][:], func=AF.Sqrt, scale=1.0 / 1536,
                                                   bias=epsb[:, 0:1]), [Bpb[2], Bconst], [Borst])
                S.op("dve", lambda h: h.reciprocal(out=orst[:], in_=orst[:]), [Borst], [Borst])
                for ctile in range(12):
                    S.op("dve", lambda h, ctile=ctile: h.scalar_tensor_tensor(
                        out=cqb[:, ctile, :], in0=cqb[:, ctile, :], scalar=gq[:, ctile:ctile + 1], in1=orst[:],
                        op0=ALU.mult, op1=ALU.mult), [Bcqb, Borst, Bconst], [Bcqb])
                S.dma("sp", lambda h, o0=o0: h.dma_start(
                    out=s_cqn[:, :, o0:o0 + 512].rearrange("c p t -> p c t"), in_=cqb[:]), [Bcqb], [Bcqn])
                for hd in range(NH):
                    w, Bw = load_q(36 + hd)
                    ps, Bps = fm_mm(w, Bw)
                    S.op("act", lambda h, ps=ps: h.activation(out=mgb[:], in_=ps[:], func=AF.Silu), [Bps], [Bmgb])
                    S.dma("sp", lambda h, hd=hd, o0=o0: h.dma_start(out=s_mg[hd, :, o0:o0 + 512], in_=mgb[:]),
                          [Bmgb], [Bmg])

        S.barrier()
        if stop == 2:
            S.wait_all("sp", S.dmabufs)
            return nc
        with ExitStack() as s2:
            psum_alloc(s2, 8, 0, 3)
            pb, Bpb = P["pb"], P["Bpb"]
            latT = sbuf(s2, "latT", [128, 4, L], BF16)
            krT = sbuf(s2, "krT", [64, L], BF16)
            cqn = sbuf(s2, "cqn", [128, 12, BLK], BF16)
            Blatsb, Bkrsb, Bcqsb = Buf("latT"), Buf("krT"), Buf("cqn")
            S.dma("sp", lambda h: [h.dma_start(out=latT[:, c, :], in_=s_lat[c]) for c in range(4)], Blat, [Blatsb])
            S.dma("sp", lambda h: h.dma_start(out=krT[:], in_=s_lat[4, 0:64, :]), Blat, [Bkrsb])
            S.dma("sp", lambda h: h.dma_start(out=cqn[:], in_=s_cqn.rearrange("c p t -> p c t")), [Bcqn], [Bcqsb])
            rope_emit, cosT, sinT, Bcos, Bsin = rope_tables(s2, OWN0, BLK, "r2")
            rope_emit(OWN0)
            wq = [sbuf(s2, "wq%d" % i, [128, 12, 256], BF16) for i in range(2)]
            wkv = [sbuf(s2, "wkv%d" % i, [128, 4, 256], BF16) for i in range(2)]
            Bwq, Bwkv = [Buf("wq0"), Buf("wq1")], [Buf("wkv0"), Buf("wkv1")]
            KT = sbuf(s2, "KT", [128, L], BF16)
            Vt = sbuf(s2, "Vt", [128, NT, 128], BF16)
            qn = sbuf(s2, "qn", [128, BLK], BF16)
            qr = sbuf(s2, "qr", [64, BLK], BF16)
            qa = sbuf(s2, "qa", [64, 512], F32)
            qb_ = sbuf(s2, "qb", [64, 512], F32)
            PT = [sbuf(s2, "PT%d" % i, [128, 512], BF16) for i in range(3)]
            rden = sbuf(s2, "rden", [128, 512], F32)
            mgs = sbuf(s2, "mgs", [128, 512], BF16)
            ob = sbuf(s2, "ob", [128, 512], BF16)
            BKT, BVt, Bqn, Bqr, Bqa, Bqb, Brden, Bmgs, Bob = (Buf(n) for n in
                                                              ("KT", "Vt", "qn", "qr", "qa", "qb", "rden", "mgs", "ob"))
            BPT = [Buf("PT%d" % i) for i in range(3)]
            pti = {"i": 0}
            SCALE = float(192 ** -0.5)
            for hd in range(NH):
                i = hd % 2
                S.dma("sp", lambda h, i=i, hd=hd: h.dma_start(out=wq[i][:], in_=s_wuq[hd]), [Bsw["wuq"]], [Bwq[i]])
                S.dma("sp", lambda h, i=i, hd=hd: h.dma_start(out=wkv[i][:], in_=s_wukv[hd]), [Bsw["wukv"]], [Bwkv[i]])
                for tb in range(L // 512):
                    ps, Bps = bank()

                    def mm(h, ps=ps, tb=tb, i=i):
                        ins = None
                        for kc in range(4):
                            ins = h.matmul(ps[:], lhsT=wkv[i][:, kc, 0:128], rhs=latT[:, kc, tb * 512:(tb + 1) * 512],
                                           start=(kc == 0), stop=(kc == 3))
                        return ins
                    S.op("pe", mm, [Bwkv[i], Blatsb], [Bps])
                    S.op("dve", lambda h, ps=ps, tb=tb: h.tensor_copy(out=KT[:, tb * 512:(tb + 1) * 512], in_=ps[:]),
                         [Bps], [BKT])
                for tb in range(NT // 4):
                    ps, Bps = bank()

                    def mm(h, ps=ps, tb=tb, i=i):
                        ins = None
                        for j in range(4):
                            tk = tb * 4 + j
                            for kc in range(4):
                                ins = h.matmul(ps[:, j * 128:(j + 1) * 128], lhsT=latT[:, kc, tk * 128:(tk + 1) * 128],
                                               rhs=wkv[i][:, kc, 128:256], start=(kc == 0), stop=(kc == 3))
                        return ins
                    S.op("pe", mm, [Bwkv[i], Blatsb], [Bps])
                    S.op("act", lambda h, ps=ps, tb=tb: h.activation(
                        out=Vt[:, tb * 4:(tb + 1) * 4, :].rearrange("p a b -> p (a b)"), in_=ps[:], func=AF.Copy),
                        [Bps], [BVt])
                for qt in range(2):
                    ps, Bps = bank()

                    def mm(h, ps=ps, qt=qt, i=i):
                        ins = None
                        for kc in range(12):
                            ins = h.matmul(ps[:], lhsT=wq[i][:, kc, 0:128], rhs=cqn[:, kc, qt * 512:(qt + 1) * 512],
                                           start=(kc == 0), stop=(kc == 11))
                        return ins
                    S.op("pe", mm, [Bwq[i], Bcqsb], [Bps])
                    S.op("act", lambda h, ps=ps, qt=qt: h.activation(out=qn[:, qt * 512:(qt + 1) * 512], in_=ps[:],
                                                                     func=AF.Copy), [Bps], [Bqn])
                    for half, dst, Bd, tab, Btab in ((0, qa, Bqa, cosT, Bcos), (1, qb_, Bqb, sinT, Bsin)):
                        ps, Bps = bank()

                        def mm(h, ps=ps, qt=qt, i=i, half=half):
                            ins = None
                            for kc in range(12):
                                ins = h.matmul(ps[0:64, :], lhsT=wq[i][:, kc, 128 + half * 64:192 + half * 64],
                                               rhs=cqn[:, kc, qt * 512:(qt + 1) * 512], start=(kc == 0), stop=(kc == 11))
                            return ins
                        S.op("pe", mm, [Bwq[i], Bcqsb], [Bps])
                        S.op("dve", lambda h, ps=ps, dst=dst, tab=tab, qt=qt: h.tensor_tensor(
                            out=dst[:], in0=ps[0:64, :], in1=tab[:, qt * 512:(qt + 1) * 512], op=ALU.mult),
                            [Bps, Btab], [Bd])
                    S.op("dve", lambda h, qt=qt: h.tensor_tensor(out=qr[:, qt * 512:(qt + 1) * 512], in0=qa[:], in1=qb_[:],
                                                                  op=ALU.add), [Bqa, Bqb], [Bqr])
                for qt in range(2):
                    q0 = qt * 512
                    nfull = (OWN0 + q0) // 128
                    po, Bpo = pb[5], Bpb[5]
                    pd, Bpd = pb[6], Bpb[6]
                    blocks = [(kb, 0) for kb in range(nfull)] + [(nfull + d, d * 128) for d in range(4)]
                    sc = {}

                    def scores(bi):
                        kb, c0 = blocks[bi]
                        k0 = kb * 128
                        ps, Bps = bank(0, 3)

                        def mm(h, ps=ps, k0=k0, c0=c0, q0=q0):
                            h.matmul(ps[:, c0:512], lhsT=KT[:, k0:k0 + 128], rhs=qn[:, q0 + c0:q0 + 512],
                                     start=True, stop=False)
                            return h.matmul(ps[:, c0:512], lhsT=krT[:, k0:k0 + 128], rhs=qr[:, q0 + c0:q0 + 512],
                                            start=False, stop=True)
                        S.op("pe", mm, [BKT, Bkrsb, Bqn, Bqr], [Bps])
                        sc[bi] = (ps, Bps)
                    for bi in range(min(2, len(blocks))):
                        scores(bi)
                    for bi, (kb, c0) in enumerate(blocks):
                        ps, Bps = sc.pop(bi)
                        pi = pti["i"] % 3
                        pti["i"] += 1
                        P_, BP_ = PT[pi], BPT[pi]
                        S.op("act", lambda h, ps=ps, P_=P_, c0=c0, kb=kb: h.activation(
                            out=P_[:, c0:512], in_=ps[:, c0:512], func=AF.Exp, scale=SCALE,
                            bias=maskbt[:, kb:kb + 1]), [Bps, Bconst], [BP_])
                        if kb >= nfull:
                            S.op("dve", lambda h, P_=P_, c0=c0: h.memset(P_[64:128, c0:c0 + 64], 0.0), [], [BP_])
                        if bi + 2 < len(blocks):
                            scores(bi + 2)
                        first, last = (bi == 0), (bi == len(blocks) - 1)

                        def mmpv(h, P_=P_, kb=kb, c0=c0, first=first, last=last):
                            h.matmul(po[:, c0:512], lhsT=Vt[:, kb, :], rhs=P_[:, c0:512], start=first, stop=last,
                                     skip_group_check=True)
                            return h.matmul(pd[:, c0:512], lhsT=onesb[:], rhs=P_[:, c0:512], start=first, stop=last,
                                            skip_group_check=True)
                        S.op("pe", mmpv, [BVt, BP_, Bconst], [Bpo, Bpd])
                    S.op("dve", lambda h: h.reciprocal(out=rden[:], in_=pd[:]), [Bpd], [Brden])
                    S.dma("sp", lambda h, hd=hd, q0=q0: h.dma_start(out=mgs[:], in_=s_mg[hd, :, q0:q0 + 512]), [Bmg], [Bmgs])
                    S.op("dve", lambda h: h.tensor_tensor(out=rden[:], in0=rden[:], in1=mgs[:], op=ALU.mult),
                         [Brden, Bmgs], [Brden])
                    S.op("dve", lambda h: h.tensor_tensor(out=ob[:], in0=po[:], in1=rden[:], op=ALU.mult),
                         [Bpo, Brden], [Bob])
                    S.dma("sp", lambda h, hd=hd, q0=q0: h.dma_start(out=s_oT[16 + hd, :, q0:q0 + 512], in_=ob[:]),
                          [Bob], [BoT[16 + hd]])

        S.barrier()
        if stop == 3:
            S.wait_all("sp", S.dmabufs)
            return nc
        By = Buf("y")
        with ExitStack() as s3:
            psum_alloc(s3, 8, 0, 3)
            oT = sbuf(s3, "oT", [128, KC, 512], BF16)
            wo = [sbuf(s3, "wo%d" % i, [128, KC, 512], BF16) for i in range(2)]
            mix = sbuf(s3, "mix", [128, 4, D], F32)
            ggb = sbuf(s3, "ggb", [128, D], F32)
            xo = sbuf(s3, "xo", [128, D], F32)
            junk = sbuf(s3, "junk", [128, D], BF16)
            ss3 = sbuf(s3, "ss3", [128, 1], F32)
            BoTsb, Bmix, Bggb, Bxo, Bjunk, Bss3 = (Buf(n) for n in ("oTsb", "mix", "ggb", "xo", "junk", "ss3"))
            Bwo = [Buf("wo0"), Buf("wo1")]
            S.dma("sp", lambda h: h.dma_start(out=ggb[:], in_=s_gg[0:1, :].partition_broadcast(128)), [Bsgg], [Bggb])
            for so in range(2):
                o0 = so * 512
                S.dma("sp", lambda h, o0=o0: h.dma_start(
                    out=oT[:], in_=s_oT[:, :, o0:o0 + 512].rearrange("c p t -> p c t")), BoT, [BoTsb])
                for ctile in range(8):
                    i = ctile % 2
                    S.dma("sp", lambda h, i=i, ctile=ctile: h.dma_start(out=wo[i][:], in_=s_wout[ctile]),
                          [Bsw["wout"]], [Bwo[i]])
                    for tt in range(4):
                        ps, Bps = bank(0, 7)

                        def mm(h, ps=ps, tt=tt, i=i):
                            ins = None
                            for fc in range(KC):
                                ins = h.matmul(ps[:], lhsT=oT[:, fc, tt * 128:(tt + 1) * 128], rhs=wo[i][:, fc, :],
                                               start=(fc == 0), stop=(fc == KC - 1))
                            return ins
                        S.op("pe", mm, [BoTsb, Bwo[i]], [Bps])
                        S.op("act", lambda h, ps=ps, tt=tt, ctile=ctile: h.activation(
                            out=mix[:, tt, ctile * 512:(ctile + 1) * 512], in_=ps[:], func=AF.Copy), [Bps], [Bmix])
                for tt in range(4):
                    r0 = o0 + tt * 128
                    S.op("act", lambda h, tt=tt: h.activation(out=junk[:], in_=mix[:, tt, :], func=AF.Square,
                                                              accum_out=ss3[:, 0:1]), [Bmix], [Bjunk, Bss3])
                    S.op("act", lambda h: h.activation(out=ss3[:, 0:1], in_=ss3[:, 0:1], func=AF.Sqrt, scale=1.0 / D,
                                                       bias=epsb[:, 0:1]), [Bss3, Bconst], [Bss3])
                    S.op("dve", lambda h: h.reciprocal(out=ss3[:, 0:1], in_=ss3[:, 0:1]), [Bss3], [Bss3])
                    S.dma("sp", lambda h, r0=r0: h.dma_start(out=xo[:], in_=xl[OWN0 + r0:OWN0 + r0 + 128, :]), [], [Bxo])
                    S.op("dve", lambda h, tt=tt: h.scalar_tensor_tensor(
                        out=mix[:, tt, :], in0=mix[:, tt, :], scalar=ss3[:, 0:1], in1=ggb[:], op0=ALU.mult, op1=ALU.mult),
                        [Bmix, Bss3, Bggb], [Bmix])
                    S.op("dve", lambda h, tt=tt: h.tensor_tensor(out=xo[:], in0=xo[:], in1=mix[:, tt, :], op=ALU.add),
                         [Bxo, Bmix], [Bxo])
                    S.dma("sp", lambda h, r0=r0: h.dma_start(out=y[r0:r0 + 128, :], in_=xo[:]), [Bxo], [By])
            S.wait_all("sp", [By])
            if dev:
                S.wait_all("sp", Blat + [BSsn, Bcqn, Bmg] + BoT)
    return nc


def _tile_cols(W, tw):
    K, N = W.shape
    n = N // tw
    return np.ascontiguousarray(W.reshape(K // 128, 128, n, tw).transpose(2, 1, 0, 3))


def prep_shared(inp):
    f = lambda a: np.asarray(a, dtype=np.float32)
    w_in = f(inp["w_in"])
    o = np.cumsum([0, 1024, 1024, 2048, 16, 2048, 1536, 512, 64, 2048])
    gq_, gk_, gv_, alr_, gg_, cq_, ckv_, kr_, mg_ = [w_in[:, o[i]:o[i + 1]] for i in range(9)]
    swap = np.concatenate([np.arange(32, 64), np.arange(0, 32)])
    sh = {}
    sh["wkfm"] = _tile_cols(np.concatenate([ckv_, kr_, kr_[:, swap]], 1), 128)
    sh["walr"] = np.ascontiguousarray(alr_.reshape(KC, 128, 16).transpose(1, 0, 2))
    sh["wktm"] = _tile_cols(np.concatenate([gk_, gv_], 1), 256)
    sh["wqfm"] = _tile_cols(np.concatenate([gq_, gg_, cq_, mg_], 1), 128)
    wuq = f(inp["w_uq"]).reshape(1536, NH, 192)
    wuq = np.concatenate([wuq[:, :, 0:128], wuq[:, :, 128:192], wuq[:, :, 128:192][:, :, swap]], 2)
    sh["wuq"] = np.ascontiguousarray(wuq.reshape(12, 128, NH, 256).transpose(2, 1, 0, 3))
    wukv = f(inp["w_ukv"]).reshape(512, NH, 256)
    sh["wukv"] = np.ascontiguousarray(wukv.reshape(4, 128, NH, 256).transpose(2, 1, 0, 3))
    sh["wout"] = _tile_cols(f(inp["w_out"]), 512)
    sh["wada"] = _tile_cols(f(inp["w_ada"]), 512)
    sh["cvec"] = np.ascontiguousarray(f(inp["c"]).reshape(KC, 128).T)
    sh["bada"] = f(inp["b_ada"]).reshape(1, -1)
    sh["gpre"] = f(inp["g_pre"]).reshape(1, -1)
    sh["gpost"] = f(inp["g_post"]).reshape(1, -1)
    sh["wup"] = np.concatenate([f(inp["w_alpha_up"]), f(inp["b_alpha"]).reshape(1, -1)], 0)
    sh["gq"] = np.ascontiguousarray(f(inp["g_q_norm"]).reshape(12, 128).T)
    sh["gkv"] = np.ascontiguousarray(f(inp["g_kv_norm"]).reshape(4, 128).T)
    sh["ggla"] = f(inp["g_gla_out"]).reshape(128, 1)
    invf = (10000.0 ** (-np.arange(32, dtype=np.float32) / 32)).astype(np.float32)
    sh["invf"] = np.concatenate([invf, invf]).reshape(64, 1)
    sh["sgn"] = np.concatenate([-np.ones(32, np.float32), np.ones(32, np.float32)]).reshape(64, 1)
    sh["ident"] = np.eye(128, dtype=np.float32)
    s_ = np.arange(128)[:, None]
    t_ = np.arange(128)[None, :]
    sh["tri"] = (((s_ > t_) & (s_ // 64 == t_ // 64)).astype(np.float32) * np.float32(-1.0 / 16)).astype(np.float32)
    sh["cind"] = ((s_ // 64 == np.arange(2)[None, :]).astype(np.float32) * np.float32(-1.0 / 16)).astype(np.float32)
    return sh


def prep_core(inp, r, nblk):
    L = nblk * BLK
    x = np.asarray(inp["x"], dtype=np.float32)[0]
    pos = np.asarray(inp["positions"], dtype=np.int32)[0]
    n_real = (r + 1) * BLK
    off = L - n_real
    xl = np.zeros((L, D), np.float32)
    xl[off:] = x[:n_real]
    pl = np.zeros((L,), np.int32)
    pl[off:] = pos[:n_real]
    tokv = (np.arange(L) >= off)
    valid = np.ascontiguousarray(tokv.reshape(L // 128, 128).T.astype(np.float32))
    return {"xl": xl, "pos64": np.ascontiguousarray(np.broadcast_to(pl, (64, L))),
            "valid": valid, "maskb": np.where(valid > 0, np.float32(0.0), np.float32(NEG)).astype(np.float32)}


def run(inp, ncores, dev=False, stop=99):
    nblk = ncores
    nc = build(nblk, dev=dev, stop=stop)
    sh = prep_shared(inp)
    in_maps = []
    for r in range(ncores):
        m = dict(sh)
        m.update(prep_core(inp, r, nblk))
        in_maps.append(m)
    res = run_bass_kernel_spmd(nc, in_maps, core_ids=list(range(ncores)))
    out = np.concatenate([np.asarray(res.results[r]["y"], dtype=np.float32) for r in range(ncores)], 0)
    if dev:
        return out[None], res.results
    return out[None]


def kernel(**inputs):
    return run(inputs, 8)
```
